# Optimizing a Trainium2 kernel written in Bass

```python
import math
import jax, jax.numpy as jnp
from jax import lax
import numpy as np


D_MODEL = 1024
BATCH = 4
SEQ = 4096
DEPTH = 4

HEAD_DIM = 64
A_GROUPS = 4
A_WIDTH = A_GROUPS * HEAD_DIM
CHUNK = 128
B_HEADS = 6
B_WIDTH = B_HEADS * HEAD_DIM
DIL_PAIRS = ((128, 1), (512, 4), (2048, 16))
C_HEADS_PER_GROUP = 2
C_GROUPS = len(DIL_PAIRS)
C_HEADS = C_GROUPS * C_HEADS_PER_GROUP
C_WIDTH = C_HEADS * HEAD_DIM
C_DILATIONS = tuple(d for (w, d) in DIL_PAIRS for _ in range(C_HEADS_PER_GROUP))
N_OFFSETS = DIL_PAIRS[0][0] // DIL_PAIRS[0][1] + 1
MIX_WIDTH = A_WIDTH + B_WIDTH + C_WIDTH
IN_WIDTH = 2 * A_WIDTH + 3 * B_WIDTH + 3 * C_WIDTH
Q_BLOCK = 128
D_FF = 2816
CONV_WIDTH = 3
ROPE_THETA = 10000.0
EPS = 1e-6
N_MOD = 6

kernel_name = 'hybrid_sgu_stickbreak_dilated_convffn_adaln'


def _rmsnorm(x, g):
    x32 = x.astype(jnp.float32)
    y = x32 * lax.rsqrt(jnp.mean(x32 * x32, axis=-1, keepdims=True) + EPS)
    return (y * g.astype(jnp.float32)).astype(x.dtype)


def _rope_tables(positions):
    inv_freq = ROPE_THETA ** (-jnp.arange(0, HEAD_DIM, 2, dtype=jnp.float32) / HEAD_DIM)
    ang = positions.astype(jnp.float32)[..., None] * inv_freq
    return jnp.cos(ang)[:, :, None, :], jnp.sin(ang)[:, :, None, :]


def _rope(x, cos, sin):
    x32 = x.astype(jnp.float32)
    x1, x2 = jnp.split(x32, 2, axis=-1)
    return jnp.concatenate([x1 * cos - x2 * sin, x2 * cos + x1 * sin], axis=-1).astype(x.dtype)


def _spatial_gating(u, v, g_sgu, w_sp, b_sp):
    bsz, seq, _ = u.shape
    n_chunks = seq // CHUNK
    u = jax.nn.gelu(u)
    v = jax.nn.gelu(v).reshape(bsz, n_chunks, CHUNK, A_GROUPS, HEAD_DIM)
    v = _rmsnorm(v, g_sgu.reshape(A_GROUPS, HEAD_DIM))
    causal = jnp.tril(jnp.ones((CHUNK, CHUNK), dtype=bool))
    w_causal = jnp.where(causal[None], w_sp, 0.0).astype(v.dtype)
    mixed = jnp.einsum('gts,bnsgc->bntgc', w_causal, v) + b_sp.T[:, :, None].astype(v.dtype)
    return u * mixed.reshape(bsz, seq, A_WIDTH)


def _stick_breaking(q, k, v):
    bsz, seq, n_heads, dh = q.shape
    n_blocks = seq // Q_BLOCK
    scale = dh ** -0.5
    k_pos = jnp.arange(seq)
    q_blocks = q.reshape(bsz, n_blocks, Q_BLOCK, n_heads, dh).transpose(1, 0, 2, 3, 4)

    def one_block(args):
        q_blk, blk = args
        q_pos = blk * Q_BLOCK + jnp.arange(Q_BLOCK)
        before = (k_pos[None, :] < q_pos[:, None])[None, None]
        z = jnp.einsum('bqhd,bkhd->bhqk', q_blk, k).astype(jnp.float32) * scale
        log_beta = jax.nn.log_sigmoid(z)
        log_stay = jnp.where(before, jax.nn.log_sigmoid(-z), 0.0)
        tail = lax.cumsum(log_stay, axis=3, reverse=True) - log_stay
        weight = jnp.where(before, jnp.exp(log_beta + tail), 0.0)
        return jnp.einsum('bhqk,bkhd->bqhd', weight.astype(v.dtype), v)

    out = lax.map(one_block, (q_blocks, jnp.arange(n_blocks)))
    return out.transpose(1, 0, 2, 3, 4).reshape(bsz, seq, n_heads * dh)


def _dilated_window(q, k, v):
    bsz, seq, n_heads, dh = q.shape
    n_blocks = seq // Q_BLOCK
    scale = dh ** -0.5
    dil = jnp.array(C_DILATIONS, dtype=jnp.int32)
    offsets = jnp.arange(N_OFFSETS, dtype=jnp.int32)
    head_idx = jnp.arange(n_heads)[None, None, :]
    q_blocks = q.reshape(bsz, n_blocks, Q_BLOCK, n_heads, dh).transpose(1, 0, 2, 3, 4)

    def one_block(args):
        q_blk, blk = args
        q_pos = blk * Q_BLOCK + jnp.arange(Q_BLOCK, dtype=jnp.int32)
        idx = q_pos[:, None, None] - offsets[None, :, None] * dil[None, None, :]
        valid = idx >= 0
        idx = jnp.maximum(idx, 0)
        k_g = k[:, idx, head_idx, :]
        v_g = v[:, idx, head_idx, :]
        z = jnp.einsum('bqhd,bqmhd->bqhm', q_blk, k_g).astype(jnp.float32) * scale
        z = jnp.where(valid.transpose(0, 2, 1)[None], z, -jnp.inf)
        z_max = jnp.max(z, axis=-1, keepdims=True)
        p = jnp.exp(z - z_max)
        denom = jnp.sum(p, axis=-1)
        o = jnp.einsum('bqhm,bqmhd->bqhd', p.astype(v.dtype), v_g) / denom[..., None]
        return o.astype(q.dtype), z_max[..., 0] + jnp.log(denom)

    o, lse = lax.map(one_block, (q_blocks, jnp.arange(n_blocks)))
    o = o.transpose(1, 0, 2, 3, 4).reshape(bsz, seq, C_GROUPS, C_HEADS_PER_GROUP, dh)
    lse = lse.transpose(1, 0, 2, 3).reshape(bsz, seq, C_GROUPS, C_HEADS_PER_GROUP)
    alpha = jax.nn.softmax(lse, axis=2)
    return (o * alpha[..., None].astype(o.dtype)).reshape(bsz, seq, n_heads * dh)


def _causal_dwconv(h, w, b):
    seq = h.shape[1]
    hp = jnp.pad(h, ((0, 0), (CONV_WIDTH - 1, 0), (0, 0)))
    out = b.astype(h.dtype)
    for i in range(CONV_WIDTH):
        out = out + w[i].astype(h.dtype) * hp[:, i:i + seq]
    return out


def setup_inputs(seed: int = 0) -> dict:
    key = jax.random.key(seed)
    ks = jax.random.split(key, 17)
    f32 = jnp.float32

    def nrm(k, shape, s):
        return jax.random.normal(k, shape, f32) * s

    x = nrm(ks[0], (BATCH, SEQ, D_MODEL), 1.0)
    c = nrm(ks[1], (BATCH, D_MODEL), 1.0)
    offset = jax.random.randint(ks[2], (BATCH, 1), 0, 1024, dtype=jnp.int32)
    positions = jnp.arange(SEQ, dtype=jnp.int32)[None, :] + offset
    w_ada = nrm(ks[3], (DEPTH, D_MODEL, N_MOD * D_MODEL), 0.2 * D_MODEL ** -0.5)
    b_ada = nrm(ks[4], (DEPTH, N_MOD * D_MODEL), 0.02)
    g_mix = 1.0 + nrm(ks[5], (DEPTH, D_MODEL), 0.05)
    w_in = nrm(ks[6], (DEPTH, D_MODEL, IN_WIDTH), D_MODEL ** -0.5)
    g_sgu = 1.0 + nrm(ks[7], (DEPTH, A_WIDTH), 0.05)
    w_sp = nrm(ks[8], (DEPTH, A_GROUPS, CHUNK, CHUNK), CHUNK ** -0.5)
    b_sp = 1.0 + nrm(ks[9], (DEPTH, A_GROUPS, CHUNK), 0.1)
    w_out = nrm(ks[10], (DEPTH, MIX_WIDTH, D_MODEL), MIX_WIDTH ** -0.5)
    g_ffn = 1.0 + nrm(ks[11], (DEPTH, D_MODEL), 0.05)
    w_up = nrm(ks[12], (DEPTH, D_MODEL, 2 * D_FF), D_MODEL ** -0.5)
    conv_w = nrm(ks[13], (DEPTH, CONV_WIDTH, 2 * D_FF), CONV_WIDTH ** -0.5)
    conv_b = nrm(ks[14], (DEPTH, 2 * D_FF), 0.02)
    w_down = nrm(ks[15], (DEPTH, D_FF, D_MODEL), D_FF ** -0.5)
    g_final = 1.0 + nrm(ks[16], (D_MODEL,), 0.05)
    return {'x': x, 'c': c, 'positions': positions, 'w_ada': w_ada, 'b_ada': b_ada,
            'g_mix': g_mix, 'w_in': w_in, 'g_sgu': g_sgu, 'w_sp': w_sp, 'b_sp': b_sp,
            'w_out': w_out, 'g_ffn': g_ffn, 'w_up': w_up, 'conv_w': conv_w,
            'conv_b': conv_b, 'w_down': w_down, 'g_final': g_final}


def reference(x, c, positions, w_ada, b_ada, g_mix, w_in, g_sgu, w_sp, b_sp, w_out,
              g_ffn, w_up, conv_w, conv_b, w_down, g_final):
    bsz, seq, _ = x.shape
    cos, sin = _rope_tables(positions)
    splits = np.cumsum([A_WIDTH, A_WIDTH, B_WIDTH, B_WIDTH, B_WIDTH, C_WIDTH, C_WIDTH]).tolist()
    c_act = jax.nn.silu(c)
    for l in range(DEPTH):
        mod = (c_act @ w_ada[l] + b_ada[l])[:, None, :]
        shift1, scale1, gate1, shift2, scale2, gate2 = jnp.split(mod, N_MOD, axis=-1)
        h = _rmsnorm(x, g_mix[l]) * (1.0 + scale1) + shift1
        proj = h @ w_in[l]
        a_u, a_v, b_q, b_k, b_v, c_q, c_k, c_v = jnp.split(proj, splits, axis=-1)
        y_a = _spatial_gating(a_u, a_v, g_sgu[l], w_sp[l], b_sp[l])
        y_b = _stick_breaking(b_q.reshape(bsz, seq, B_HEADS, HEAD_DIM),
                              b_k.reshape(bsz, seq, B_HEADS, HEAD_DIM),
                              b_v.reshape(bsz, seq, B_HEADS, HEAD_DIM))
        y_c = _dilated_window(_rope(c_q.reshape(bsz, seq, C_HEADS, HEAD_DIM), cos, sin),
                              _rope(c_k.reshape(bsz, seq, C_HEADS, HEAD_DIM), cos, sin),
                              c_v.reshape(bsz, seq, C_HEADS, HEAD_DIM))
        y = jnp.concatenate([y_a, y_b, y_c], axis=-1) @ w_out[l]
        x = x + (1.0 + gate1) * y
        h = _rmsnorm(x, g_ffn[l]) * (1.0 + scale2) + shift2
        up = _causal_dwconv(h @ w_up[l], conv_w[l], conv_b[l])
        gate, val = jnp.split(up, 2, axis=-1)
        x = x + (1.0 + gate2) * ((jax.nn.silu(gate) * val) @ w_down[l])
    return _rmsnorm(x, g_final)
```

```python
import numpy as np
import ml_dtypes
from contextlib import ExitStack
import concourse.bass as bass
import concourse.mybir as mybir
from concourse.bass_utils import run_bass_kernel_spmd

F32 = mybir.dt.float32
BF16 = mybir.dt.bfloat16
I32 = mybir.dt.int32
AF = mybir.ActivationFunctionType
ALU = mybir.AluOpType
AX = mybir.AxisListType

D = 1024
KC = 8
IN_W = 2816
D_FF = 2816
NJ = 22
EPS = 1e-6
DILS = (1, 4, 16)


class Buf:
    __slots__ = ("name", "w", "r")

    def __init__(self, name):
        self.name = name
        self.w = None
        self.r = []


class Sched:
    ENG = ("pe", "act", "dve", "pool")

    def __init__(self, nc, es, W=16000, R=5, DP=16):
        self.nc = nc
        self.es = es
        self.eng = {"pe": nc.tensor, "act": nc.scalar, "dve": nc.vector, "pool": nc.gpsimd,
                    "sp": nc.sync}
        self.W, self.R, self.DP = W, R, DP
        self.esem = {e: [es.enter_context(nc.semaphore(f"s_{e}{i}")) for i in range(R)]
                     for e in self.ENG}
        self.cnt = {e: 0 for e in self.ENG}
        self.pend = {e: 0 for e in self.ENG}
        self.dq = ("sp", "pool")
        self.dsem = {q: [es.enter_context(nc.semaphore(f"d_{q}{i}")) for i in range(DP)]
                     for q in self.dq}
        self.dcnt = {q: 0 for q in self.dq}
        streams = self.ENG + ("sp",)
        self.we = {w: {e: 0 for e in self.ENG} for w in streams}
        self.wd = {w: {} for w in streams}
        self.nwaits = 0
        self.nins = 0

    def _wait(self, w, ev):
        if ev is None:
            return
        if ev[0] == "c":
            if self.wc[w] >= ev[1]:
                return
            self.wc[w] = ev[1]
            self.eng[w].wait_ge(self.csem, ev[1])
            self.nwaits += 1
            return
        if ev[0] == "e":
            _, e, n = ev
            if e == w and w == "pe":
                return
            if self.we[w][e] >= n:
                return
            if e == w and n > self.cnt[e]:
                raise RuntimeError("self-wait on future signal")
            self.we[w][e] = n
            slot = ((n - 1) // self.W) % self.R
            val = (n - 1) % self.W + 1
            self.eng[w].wait_ge(self.esem[e][slot], val)
            self.nwaits += 1
        else:
            _, q, k = ev
            key = (q, k % self.DP)
            val = 16 * (k // self.DP + 1)
            if self.wd[w].get(key, 0) >= val:
                return
            self.wd[w][key] = val
            self.eng[w].wait_ge(self.dsem[q][k % self.DP], val)
            self.nwaits += 1

    def _deps(self, w, reads, writes):
        for b in reads:
            self._wait(w, b.w)
        for b in writes:
            self._wait(w, b.w)
            for r in b.r:
                self._wait(w, r)

    def _commit(self, ev, reads, writes):
        for b in reads:
            b.r.append(ev)
            if len(b.r) > 64:
                b.r = b.r[-48:]
        for b in writes:
            b.w = ev
            b.r = []

    def op(self, e, fn, reads=(), writes=(), signal=True):
        self._deps(e, reads, writes)
        ins = fn()
        self.nins += 1
        if signal:
            n = self.cnt[e] + 1
            self.cnt[e] = n
            slot = ((n - 1) // self.W) % self.R
            assert n <= self.W * self.R, "semaphore ring exhausted"
            ins.then_inc(self.esem[e][slot], 1)
            self.pend[e] = 0
            ev = ("e", e, n)
        else:
            assert e == "pe"
            self.pend[e] += 1
            ev = ("e", e, self.cnt[e] + 1)
        self._commit(ev, reads, writes)
        return ev

    def dma(self, out, in_, reads=(), writes=(), q="sp", **kw):
        k = self.dcnt[q]
        if k >= self.DP:
            self._wait(q, ("d", q, k - self.DP))
        self._deps(q, reads, writes)
        ins = self.eng[q].dma_start(out=out, in_=in_, **kw)
        ins.then_inc(self.dsem[q][k % self.DP], 16)
        self.dcnt[q] = k + 1
        self.nins += 1
        ev = ("d", q, k)
        self._commit(ev, reads, writes)
        return ev

    def collective(self, kind, op, groups, ins, outs, reads=(), writes=()):
        if not hasattr(self, "csem"):
            self.csem = self.es.enter_context(self.nc.semaphore("s_cc"))
            self.ccnt = 0
            self.wc = {w: 0 for w in self.ENG + ("sp",)}
        self._deps("pool", reads, writes)
        ins_ = self.nc.gpsimd.collective_compute(kind, op, replica_groups=groups, ins=ins, outs=outs)
        ins_.then_inc(self.csem, 1)
        self.ccnt += 1
        self.nins += 1
        ev = ("c", self.ccnt)
        self._commit(ev, reads, writes)
        return ev

    def fence(self, cc=True):
        assert all(v == 0 for v in self.pend.values())
        evs = [("e", e, self.cnt[e]) for e in self.ENG if self.cnt[e] > 0]
        for q in self.dq:
            k = self.dcnt[q]
            for i in range(max(0, k - self.DP), k):
                evs.append(("d", q, i))
        if cc and getattr(self, "ccnt", 0):
            evs.append(("c", self.ccnt))
        for w in self.ENG + ("sp",):
            for ev in evs:
                self._wait(w, ev)

    def finish(self, w="sp"):
        self.fence()


class Arena:
    def __init__(self, nc, es, nwords):
        self.t = es.enter_context(nc.sbuf_tensor("arena", [128, nwords], F32))
        self.nwords = nwords
        self.off = 0
        self.n = 0

    def reset(self, off=0):
        self.off = off

    def alloc(self, shape, dt, parts=128, nbuf=1):
        nel = int(np.prod(shape))
        nw = (nel * (2 if dt == BF16 else 4) + 3) // 4
        nw = (nw + 7) // 8 * 8
        res = []
        for _ in range(nbuf):
            assert self.off + nw <= self.nwords, f"arena overflow {self.off}+{nw}>{self.nwords}"
            v = self.t[0:parts, self.off:self.off + nw]
            if dt == BF16:
                v = v.bitcast(BF16)
            elif dt == I32:
                v = v.bitcast(I32)
            v = v[:, 0:nel]
            if len(shape) == 2:
                v = v.rearrange("p (a b) -> p a b", a=shape[0])
            elif len(shape) == 3:
                v = v.rearrange("p (a b c) -> p a b c", a=shape[0], b=shape[1])
            self.n += 1
            res.append((v, Buf(f"a{self.n}")))
            self.off += nw
        return res


class Builder:
    def __init__(self, T, L, debug=False, arena_kib=176, ncores=8):
        self.T, self.L, self.debug = T, L, debug
        self.groups = [[2 * i, 2 * i + 1] for i in range(ncores // 2)]
        self.NG = T // 512
        self.NB = T // 128
        assert T % 2048 == 0
        self.nc = bass.Bass("TRN2", target_bir_lowering=False)
        self.arena_words = arena_kib * 256

    def _dram_in(self, name, shape, dt=F32):
        return self.nc.dram_tensor(name, list(shape), dt, kind="ExternalInput").ap()

    def _dram_scr(self, name, shape, dt):
        kind = "ExternalOutput" if (self.debug and name in self.debug) else "Internal"
        return self.nc.dram_tensor(name, list(shape), dt, kind=kind).ap()

    def declare(self):
        T, L = self.T, self.L
        i = self._dram_in
        self.xT = i("xT", [8, 128, T])
        self.cT = i("cT", [128, 8])
        self.pos = i("pos", [1, T], I32)
        self.w_ada = i("w_ada", [L, 1024, 6144])
        self.b_adaT = i("b_adaT", [L, 128, 48])
        self.g_mixT = i("g_mixT", [L, 128, 8])
        self.g_ffnT = i("g_ffnT", [L, 128, 8])
        self.g_finalT = i("g_finalT", [128, 8])
        self.w_in = i("w_in", [L, 1024, IN_W])
        self.g_sgu = i("g_sgu", [L, 1, 256])
        self.w_spT = i("w_spT", [L, 4, 128, 128])
        self.b_sp = i("b_sp", [L, 1, 512])
        self.w_out = i("w_out", [L, 1024, 1024])
        self.w_up = i("w_up", [L, 1024, 2 * D_FF])
        self.conv_wT = i("conv_wT", [L, 128, 44, 3])
        self.conv_bT = i("conv_bT", [L, 128, 44])
        self.w_down = i("w_down", [L, D_FF, 1024])
        self.c_bf = i("c_bf", [128, 128 + 128 + 2048 + 512 + 128 + 128 + 128], BF16)
        self.c_f32 = i("c_f32", [128, 128 + 64 + 4])
        self.ctxm_d = i("ctxm", [128, 1])
        s = self._dram_scr
        self.outT = self.nc.dram_tensor("outT", [8, 128, T], F32, kind="ExternalOutput").ap()
        self.XS = s("XS", [8, 128, T], F32)
        self.YT = s("YT", [8, 128, T], BF16)
        NB = self.NB
        self.QB2 = s("QB2", [384, T], BF16)
        self.QBg = s("QBg", [768, T], BF16)
        self.QB = self.QB2.rearrange("(m p) t -> m p t", p=128)
        self.QBr1 = self.QBg[384:768, :].rearrange("(m p) t -> m p t", p=128)
        self.CRY2 = s("CRY2", [6, T], BF16)
        self.CRYg = s("CRYg", [12, T], BF16)
        self.PVO = s("PVO", [3, 128, T], F32)
        self.PVC2 = [s(f"PVC2_{i}", [128, T], F32) for i in range(3)]
        self.PVCg = [s(f"PVCg_{i}", [256, T], F32) for i in range(3)]
        self.KB2 = s("KB2", [384, T], BF16)
        self.KBg = s("KBg", [768, T], BF16)
        self.VB2 = s("VB2", [NB * 128, 384], BF16)
        self.VBg = s("VBg", [2 * NB * 128, 384], BF16)
        self.QC = s("QC", [3, 128, T], BF16)
        self.KC2 = s("KC2", [384, T], BF16)
        self.KCg = s("KCg", [768, T], BF16)
        self.VC2 = s("VC2", [3 * NB * 128, 130], BF16)
        self.VCg = s("VCg", [6 * NB * 128, 130], BF16)
        self.HL = s("HL", [128, 16], BF16)
        self.HLg = s("HLg", [256, 16], BF16)
        self.KB = self.KB2.rearrange("(m p) t -> m p t", p=128)
        self.KBc = self.KBg[0:384, :].rearrange("(m p) t -> m p t", p=128)
        self.VB = self.VB2.rearrange("(b p) c -> b p c", p=128)
        self.VBc = self.VBg[0:NB * 128, :].rearrange("(b p) c -> b p c", p=128)
        self.KC = self.KC2.rearrange("(m p) t -> m p t", p=128)
        self.KCc = self.KCg[0:384, :].rearrange("(m p) t -> m p t", p=128)
        self.VC = self.VC2.rearrange("(g b p) c -> g b p c", g=3, p=128)
        self.VCc = self.VCg[0:3 * NB * 128, :].rearrange("(g b p) c -> g b p c", g=3, p=128)
        self.MT = s("MT", [NJ, 128, T], BF16)
        self.CSd = s("CSd", [128, 2, T], F32)

    def mm(self, out, lhsT, rhs, start, stop, reads, writes, signal=True):
        nc = self.nc
        return self.S.op("pe", lambda: nc.tensor.matmul(out, lhsT=lhsT, rhs=rhs, start=start, stop=stop),
                         reads=reads, writes=writes, signal=signal)

    def act(self, out, in_, func, reads, writes, **kw):
        nc = self.nc
        return self.S.op("act", lambda: nc.scalar.activation(out=out, in_=in_, func=func, **kw),
                         reads=reads, writes=writes)

    def tt(self, e, out, in0, in1, op, reads, writes):
        eng = self.S.eng[e]
        return self.S.op(e, lambda: eng.tensor_tensor(out=out, in0=in0, in1=in1, op=op), reads=reads, writes=writes)

    def ts(self, e, out, in0, s1, s2, op0, op1, reads, writes):
        eng = self.S.eng[e]
        if s2 is None:
            return self.S.op(e, lambda: eng.tensor_scalar(out=out, in0=in0, scalar1=s1, scalar2=None, op0=op0),
                             reads=reads, writes=writes)
        return self.S.op(e, lambda: eng.tensor_scalar(out=out, in0=in0, scalar1=s1, scalar2=s2, op0=op0, op1=op1),
                         reads=reads, writes=writes)

    def stt(self, out, in0, scalar, in1, op0, op1, reads, writes):
        nc = self.nc
        return self.S.op("dve", lambda: nc.vector.scalar_tensor_tensor(out=out, in0=in0, scalar=scalar, in1=in1,
                                                                       op0=op0, op1=op1), reads=reads, writes=writes)

    def cp(self, e, out, in_, reads, writes):
        eng = self.S.eng[e]
        if e == "act":
            return self.act(out, in_, AF.Copy, reads, writes)
        return self.S.op(e, lambda: eng.tensor_copy(out=out, in_=in_), reads=reads, writes=writes)

    def memset(self, e, out, val, writes):
        eng = self.S.eng[e]
        return self.S.op(e, lambda: eng.memset(out, val), writes=writes)

    @staticmethod
    def pipeline(items, stages):
        n, K = len(items), len(stages)
        for i in range(n + K - 1):
            for k, f in enumerate(stages):
                j = i - k
                if 0 <= j < n:
                    f(items[j], j)

    def phase(self, keep_h=False, cc=True):
        self.S.fence(cc=cc)
        self.A.reset(self.h_words if keep_h else 0)

    def build(self):
        nc = self.nc
        self.declare()
        with ExitStack() as es:
            self.es = es
            self.S = Sched(nc, es)
            self.A = Arena(nc, es, self.arena_words)
            self.h_words = 8 * self.T // 2
            self.ps = []
            self.pb = []
            for k in range(8):
                self.ps.append(es.enter_context(nc.psum_tensor(f"ps{k}", [128, 512], F32)))
                self.pb.append(Buf(f"ps{k}"))
            self.cbf = es.enter_context(nc.sbuf_tensor("cbf", [128, 128 + 128 + 2048 + 512 + 128 + 128 + 128], BF16))
            self.cf = es.enter_context(nc.sbuf_tensor("cf", [128, 128 + 64 + 4], F32))
            self.cact = es.enter_context(nc.sbuf_tensor("cact", [128, 8], F32))
            self.cact_bf = es.enter_context(nc.sbuf_tensor("cact_bf", [128, 8], BF16))
            self.modv = es.enter_context(nc.sbuf_tensor("modv", [128, self.L, 80], F32))
            self.gfin = es.enter_context(nc.sbuf_tensor("gfin", [128, 8], F32))
            self.ctxm = es.enter_context(nc.sbuf_tensor("ctxm_sb", [128, 1], F32))
            self.b_qg = Buf("qg")
            self.b_pvg = Buf("pvg")
            self.ictx = es.enter_context(nc.sbuf_tensor("ictx", [128, 1], F32))
            self.b_kbg = Buf("kbg")
            self.b_kcg = Buf("kcg")
            self.b_const = Buf("const")
            self.b_mod = Buf("mod")
            c = self.cbf
            self.ones = c[:, 0:128]
            self.negU = c[:, 128:256]
            self.sbmask = c[:, 256:2304].rearrange("p (a b) -> p a b", a=4)
            self.dmask = c[:, 2304:2816].rearrange("p (a b) -> p a b", a=2)
            self.zeros128 = c[:, 2816:2944]
            self.negI = c[:, 2944:3072]
            f = self.cf
            self.tri = f[:, 0:128]
            self.onesf = f[:, 128:192]
            self.invf = f[:, 192:193]
            self.sgn = f[:, 193:194]
            self.hT = self.A.t[:, 0:self.h_words].bitcast(BF16).rearrange("p (c t) -> p c t", c=8)
            self.hb = [Buf(f"h{g}") for g in range(self.NG)]
            self.hb_all = self.hb

            self.setup()
            xsrc = self.xT
            for l in range(self.L):
                self.norm(l, 1, xsrc)
                self.sgu(l)
                self.projB(l)
                self.projC(l)
                self.attB_own(l)
                self.attC(l)
                self.attB_ctx(l)
                self.outproj(l, xsrc)
                xsrc = self.XS
                self.norm(l, 2, xsrc)
                self.ffn_up(l)
                self.ffn_down(l)
            self.norm(None, 0, xsrc)
            self.S.finish()
        return nc

    def setup(self):
        nc, S, A = self.nc, self.S, self.A
        T = self.T
        bc = self.b_const
        S.dma(self.cbf[:], self.c_bf[:, :], writes=[bc])
        S.dma(self.cf[:], self.c_f32[:, :], writes=[bc])
        S.dma(self.cact[:], self.cT[:, :], writes=[bc])
        S.dma(self.gfin[:], self.g_finalT[:, :], writes=[bc])
        S.dma(self.ctxm[:], self.ctxm_d[:, :], writes=[bc])
        self.ts("dve", self.ictx[:], self.ctxm[:], -1.0, 1.0, ALU.mult, ALU.add, [bc], [bc])
        self.act(self.cact[:], self.cact[:], AF.Silu, [bc], [bc])
        self.cp("dve", self.cact_bf[:], self.cact[:], [bc], [bc])
        self.phase()
        PI = float(np.pi)
        C1 = 6.28125
        C2 = float(2 * np.pi - 6.28125)
        posi = A.alloc([512], I32, nbuf=2)
        ang = A.alloc([512], F32, nbuf=2)
        a2 = A.alloc([512], F32, nbuf=2)
        ki = A.alloc([512], I32, nbuf=2)
        kf = A.alloc([512], F32, nbuf=2)
        cs = A.alloc([2, 512], F32, nbuf=2)
        for g in range(self.NG):
            pi_, bpi = posi[g % 2]
            an, ban = ang[g % 2]
            a2_, ba2 = a2[g % 2]
            ki_, bki = ki[g % 2]
            kf_, bkf = kf[g % 2]
            cs_, bcs = cs[g % 2]
            S.dma(pi_, self.pos[:, g * 512:(g + 1) * 512].partition_broadcast(128), writes=[bpi])
            self.cp("dve", an, pi_, [bpi], [ban])
            self.ts("dve", an, an, self.invf[:, :], None, ALU.mult, None, [ban, bc], [ban])
            for which in range(2):
                if which == 0:
                    self.ts("dve", a2_, an, PI / 2, None, ALU.add, None, [ban], [ba2])
                else:
                    self.cp("dve", a2_, an, [ban], [ba2])
                self.ts("dve", ki_, a2_, float(1 / (2 * np.pi)), None, ALU.mult, None, [ba2], [bki])
                self.cp("dve", kf_, ki_, [bki], [bkf])
                self.stt(a2_, kf_, -C1, a2_, ALU.mult, ALU.add, [bkf, ba2], [ba2])
                self.stt(a2_, kf_, -C2, a2_, ALU.mult, ALU.add, [bkf, ba2], [ba2])
                self.ts("dve", a2_, a2_, 3.1415925, -3.1415925, ALU.min, ALU.max, [ba2], [ba2])
                if which == 0:
                    self.act(cs_[:, 0, :], a2_, AF.Sin, [ba2], [bcs])
                else:
                    self.act(cs_[:, 1, :], a2_, AF.Sin, [ba2, bc], [bcs], scale=self.sgn[:, :])
            S.dma(self.CSd[:, :, g * 512:(g + 1) * 512], cs_, reads=[bcs])
        self.mod(0)

    def mod_items(self, l, ps, pb):
        nc, S, A = self.nc, self.S, self.A
        wt = A.alloc([8, 512], BF16, nbuf=2)
        bad = A.alloc([48], F32)[0]
        gm = A.alloc([16], F32)[0]
        wsrc = self.w_ada[l].rearrange("(kc p) m -> p kc m", p=128)
        items = []

        def ld_small():
            S.dma(bad[0], self.b_adaT[l], writes=[bad[1]])
            S.dma(gm[0][:, 0:8], self.g_mixT[l], writes=[gm[1]])
            S.dma(gm[0][:, 8:16], self.g_ffnT[l], writes=[gm[1]])
        items.append(ld_small)

        def mk_load(blk):
            def f():
                w_, bw = wt[blk % 2]
                S.dma(w_, wsrc[:, :, blk * 512:(blk + 1) * 512], writes=[bw], q="pool")
            return f

        def mk_mm(blk, o):
            def f():
                w_, bw = wt[blk % 2]
                oc = blk * 4 + o
                for kc in range(8):
                    self.mm(ps[:, oc:oc + 1], w_[:, kc, o * 128:(o + 1) * 128], self.cact_bf[:, kc:kc + 1],
                            kc == 0, kc == 7, [bw, self.b_const], [pb], signal=(kc == 7))
            return f
        items.append(mk_load(0))
        for blk in range(12):
            if blk + 1 < 12:
                items.append(mk_load(blk + 1))
            for o in range(4):
                items.append(mk_mm(blk, o))

        def fin():
            mv = self.modv[:, l, :]
            bm = self.b_mod
            self.tt("dve", mv[:, 0:48], ps[:, 0:48], bad[0], ALU.add, [pb, bad[1]], [bm])
            self.stt(mv[:, 48:56], mv[:, 8:16], 1.0, gm[0][:, 0:8], ALU.add, ALU.mult, [bm, gm[1]], [bm])
            self.stt(mv[:, 56:64], mv[:, 32:40], 1.0, gm[0][:, 8:16], ALU.add, ALU.mult, [bm, gm[1]], [bm])
            self.ts("dve", mv[:, 64:72], mv[:, 16:24], 1.0, None, ALU.add, None, [bm], [bm])
            self.ts("dve", mv[:, 72:80], mv[:, 40:48], 1.0, None, ALU.add, None, [bm], [bm])
        items.append(fin)
        return items

    def mod(self, l):
        self.phase()
        for f in self.mod_items(l, self.ps[0], self.pb[0]):
            f()

    def norm(self, l, which, src):
        nc, S, A = self.nc, self.S, self.A
        self.phase(keep_h=(which != 0))
        xt = A.alloc([8, 512], F32, nbuf=3)
        sq = A.alloc([8, 512], BF16, nbuf=2)
        sd = A.alloc([512], F32, nbuf=2)
        tmp = A.alloc([8, 512], F32, nbuf=2)
        ot = A.alloc([8, 512], F32, nbuf=2) if which == 0 else None
        bm = self.b_mod
        NG = self.NG

        def load(g):
            S.dma(xt[g % 3][0], src[:, :, g * 512:(g + 1) * 512].rearrange("c p t -> p c t"), writes=[xt[g % 3][1]])

        def n1(g, n):
            if g + 1 < NG:
                load(g + 1)
            x_, bx = xt[g % 3]
            s_, bs = sq[g % 2]
            ps, pb = self.ps[g % 2], self.pb[g % 2]
            self.act(s_, x_, AF.Square, [bx], [bs])
            for c in range(8):
                self.mm(ps[:, :], self.ones, s_[:, c, :], c == 0, c == 7, [bs, self.b_const], [pb], signal=(c == 7))

        def n2(g, n):
            x_, bx = xt[g % 3]
            d_, bd = sd[g % 2]
            t_, bt = tmp[g % 2]
            ps, pb = self.ps[g % 2], self.pb[g % 2]
            self.act(d_, ps[:, :], AF.Sqrt, [pb], [bd], bias=EPS, scale=1.0 / D)
            S.op("dve", lambda: nc.vector.reciprocal(out=d_, in_=d_), reads=[bd], writes=[bd])
            for c in range(8):
                self.tt("dve", t_[:, c, :], x_[:, c, :], d_, ALU.mult, [bx, bd], [bt])

        def n3(g, n):
            t_, bt = tmp[g % 2]
            for c in range(8):
                if which == 0:
                    self.act(ot[g % 2][0][:, c, :], t_[:, c, :], AF.Identity, [bt, self.b_const], [ot[g % 2][1]],
                             scale=self.gfin[:, c:c + 1])
                else:
                    gs = self.modv[:, l, 48 + 8 * (which - 1) + c:48 + 8 * (which - 1) + c + 1]
                    sh = self.modv[:, l, 24 * (which - 1) + c:24 * (which - 1) + c + 1]
                    self.act(self.hT[:, c, g * 512:(g + 1) * 512], t_[:, c, :], AF.Identity, [bt, bm], [self.hb[g]],
                             scale=gs, bias=sh)
            if which == 0:
                S.dma(self.outT[:, :, g * 512:(g + 1) * 512].rearrange("c p t -> p c t"), ot[g % 2][0],
                      reads=[ot[g % 2][1]])

        load(0)
        self.pipeline(list(range(NG)), [n1, n2, n3])
        if which == 2:
            S.dma(self.HL.rearrange("p (c i) -> p c i", c=8), self.hT[:, :, self.T - 2:self.T],
                  reads=[self.hb[NG - 1]])

    def sgu(self, l):
        nc, S, A = self.nc, self.S, self.A
        self.phase(keep_h=True)
        NG = self.NG
        bc = self.b_const
        wA, bwA = A.alloc([8, 512], BF16)[0]
        bwA0 = Buf("wA0")
        wsrcA = self.w_in[l].rearrange("(kc p) m -> p kc m", p=128)
        S.dma(wA[:, :, 0:128], wsrcA[:, :, 0:128], writes=[bwA0], q="pool")
        S.dma(wA[:, :, 128:512], wsrcA[:, :, 128:512], writes=[bwA], q="pool")
        wsf, bwsf = A.alloc([4, 128], F32)[0]
        wsp, bwsp = A.alloc([4, 128], BF16)[0]
        S.dma(wsf, self.w_spT[l].rearrange("g s t -> s g t"), writes=[bwsf])
        for g in range(4):
            self.tt("dve", wsp[:, g, :], wsf[:, g, :], self.tri, ALU.mult, [bwsf, bc], [bwsp])
        bbc, bbbc = A.alloc([4, 4, 128], F32)[0]
        for tb in range(4):
            S.dma(bbc[:, :, tb, :], self.b_sp[l].rearrange("o (g t) -> o g t", g=4).partition_broadcast(128),
                  writes=[bbbc])
        gsg, bgsg = A.alloc([256], F32)[0]
        S.dma(gsg, self.g_sgu[l].partition_broadcast(128), writes=[bgsg])
        ug = A.alloc([2, 512], F32, nbuf=3)
        vg = A.alloc([256], F32, nbuf=6)
        sqv = A.alloc([256], F32, nbuf=3)
        ss = A.alloc([4], F32, nbuf=5)
        vn = A.alloc([256], BF16, nbuf=4)
        tmp = A.alloc([512], F32, nbuf=2)
        ya = A.alloc([2, 512], BF16, nbuf=2)
        items = [(tg, tb) for tg in range(NG) for tb in range(4)]

        def a1(it, n):
            tg, tb = it
            hb = self.hb[tg]
            if tb == 0:
                u_, bu = ug[tg % 3]
                for m in range(2):
                    ps, pb = self.ps[m], self.pb[m]
                    for kc in range(8):
                        self.mm(ps[:, :], wA[:, kc, 128 * m:128 * m + 128], self.hT[:, kc, tg * 512:(tg + 1) * 512],
                                kc == 0, kc == 7, [bwA0 if m == 0 else bwA, hb], [pb], signal=(kc == 7))
                    self.act(u_[:, m, :], ps[:, :], AF.Gelu_apprx_tanh, [pb], [bu])
            ps, pb = self.ps[2 + n % 2], self.pb[2 + n % 2]
            v_, bv = vg[n % 6]
            q_, bq = sqv[n % 3]
            t0 = tg * 512 + tb * 128
            for kc in range(8):
                self.mm(ps[:, 0:256], self.hT[:, kc, t0:t0 + 128], wA[:, kc, 256:512],
                        kc == 0, kc == 7, [bwA, hb], [pb], signal=(kc == 7))
            self.act(v_, ps[:, 0:256], AF.Gelu_apprx_tanh, [pb], [bv])
            self.act(q_, v_, AF.Square, [bv], [bq])

        def a2(it, n):
            q_, bq = sqv[n % 3]
            s_, bs = ss[n % 5]
            S.op("dve", lambda: nc.vector.tensor_reduce(out=s_, in_=q_.rearrange("p (g c) -> p g c", g=4),
                                                        axis=AX.X, op=ALU.add), reads=[bq], writes=[bs])

        def a3(it, n):
            s_, bs = ss[n % 5]
            self.act(s_, s_, AF.Sqrt, [bs], [bs], bias=EPS, scale=1.0 / 64)

        def a4(it, n):
            v_, bv = vg[n % 6]
            s_, bs = ss[n % 5]
            n_, bn = vn[n % 4]
            S.op("dve", lambda: nc.vector.reciprocal(out=s_, in_=s_), reads=[bs], writes=[bs])
            for g in range(4):
                self.stt(n_[:, 64 * g:64 * g + 64], v_[:, 64 * g:64 * g + 64], s_[:, g:g + 1],
                         gsg[:, 64 * g:64 * g + 64], ALU.mult, ALU.mult, [bv, bs, bgsg], [bn])

        def a5(it, n):
            tg, tb = it
            n_, bn = vn[n % 4]
            for g in range(4):
                m = g // 2
                self.mm(self.ps[4 + g][:, tb * 128:(tb + 1) * 128], n_[:, 128 * m:128 * m + 128], wsp[:, g, :],
                        True, True, [bn, bwsp], [self.pb[4 + g]])

            if tb == 3:
                a6(it, n)

        def a6(it, n):
            tg, tb = it
            u_, bu = ug[tg % 3]
            y_, by = ya[tg % 2]
            for g in range(4):
                t_, bt = tmp[g % 2]
                m, r0 = g // 2, 64 * (g % 2)
                self.tt("dve", t_[r0:r0 + 64, :], self.ps[4 + g][r0:r0 + 64, :],
                        bbc[r0:r0 + 64, g, :, :].rearrange("p a b -> p (a b)"), ALU.add, [self.pb[4 + g], bbbc], [bt])
                self.tt("pool", y_[r0:r0 + 64, m, :], t_[r0:r0 + 64, :], u_[r0:r0 + 64, m, :], ALU.mult, [bt, bu], [by])
            S.dma(self.YT[0:2, :, tg * 512:(tg + 1) * 512].rearrange("g p t -> p g t"), y_, reads=[by])

        self.pipeline(items, [a1, a2, a3, a4, a5])

    def projB(self, l):
        nc, S, A = self.nc, self.S, self.A
        self.phase(keep_h=True)
        NG = self.NG
        wsrc = self.w_in[l].rearrange("(kc p) m -> p kc m", p=128)
        wq, bwq = A.alloc([8, 768], BF16)[0]
        wv, bwv = A.alloc([8, 384], BF16)[0]
        bwq0 = Buf("wq0")
        S.dma(wq[:, :, 0:128], wsrc[:, :, 512:640], writes=[bwq0], q="pool")
        S.dma(wq[:, :, 128:768], wsrc[:, :, 640:1280], writes=[bwq], q="pool")
        S.dma(wv, wsrc[:, :, 1280:1664], writes=[bwv], q="pool")
        qk = A.alloc([6, 512], BF16, nbuf=2)
        vt = A.alloc([4, 384], BF16, nbuf=2)
        n = 0
        for tg in range(NG):
            hb = self.hb[tg]
            q_, bq = qk[tg % 2]
            for m in range(6):
                ps, pb = self.ps[n % 4], self.pb[n % 4]
                n += 1
                for kc in range(8):
                    self.mm(ps[:, :], wq[:, kc, 128 * m:128 * m + 128], self.hT[:, kc, tg * 512:(tg + 1) * 512],
                            kc == 0, kc == 7, [bwq0 if m == 0 else bwq, hb], [pb], signal=(kc == 7))
                if m < 3:
                    self.act(q_[:, m, :], ps[:, :], AF.Copy, [pb], [bq], scale=0.125)
                else:
                    self.cp("dve", q_[:, m, :], ps[:, :], [pb], [bq])
            S.dma(self.QB[:, :, tg * 512:(tg + 1) * 512].rearrange("j p t -> p j t"), q_[:, 0:3, :], reads=[bq])
            S.dma(self.KB[:, :, tg * 512:(tg + 1) * 512].rearrange("j p t -> p j t"), q_[:, 3:6, :], reads=[bq])
            v_, bv = vt[tg % 2]
            for tb in range(4):
                ps, pb = self.ps[4 + n % 4], self.pb[4 + n % 4]
                n += 1
                t0 = tg * 512 + tb * 128
                for kc in range(8):
                    self.mm(ps[:, 0:384], self.hT[:, kc, t0:t0 + 128], wv[:, kc, :],
                            kc == 0, kc == 7, [bwv, hb], [pb], signal=(kc == 7))
                self.cp("act" if tb % 2 else "dve", v_[:, tb, :], ps[:, 0:384], [pb], [bv])
            S.dma(self.VB[4 * tg:4 * tg + 4].rearrange("b p c -> p b c"), v_, reads=[bv])
        S.fence()
        S.collective("AllGather", ALU.bypass, self.groups, [self.KB2[:, :]], [self.KBg[:, :]], writes=[self.b_kbg])
        S.collective("AllGather", ALU.bypass, self.groups, [self.VB2[:, :]], [self.VBg[:, :]], writes=[self.b_kbg])

    def projC(self, l):
        nc, S, A = self.nc, self.S, self.A
        self.phase(keep_h=True, cc=False)
        NG, T = self.NG, self.T
        wsrc = self.w_in[l].rearrange("(kc p) m -> p kc m", p=128)
        wq, bwq = A.alloc([8, 768], BF16)[0]
        wv, bwv = A.alloc([8, 384], BF16)[0]
        bwq0 = Buf("wq0")
        S.dma(wq[:, :, 0:128], wsrc[:, :, 1664:1792], writes=[bwq0], q="pool")
        S.dma(wq[:, :, 128:768], wsrc[:, :, 1792:2432], writes=[bwq], q="pool")
        S.dma(wv, wsrc[:, :, 2432:2816], writes=[bwv], q="pool")
        cs = A.alloc([2, 512], F32, nbuf=3)
        qkc = A.alloc([6, 512], BF16, nbuf=2)
        qs = A.alloc([512], F32, nbuf=3)
        t1 = A.alloc([512], F32, nbuf=3)
        t2 = A.alloc([512], F32, nbuf=3)
        vc = A.alloc([4, 2, 65], BF16, nbuf=2)

        def load(g):
            S.dma(cs[g % 3][0], self.CSd[:, :, g * 512:(g + 1) * 512], writes=[cs[g % 3][1]])
        items = [(tg, m) for tg in range(NG) for m in range(6)]

        def p1(it, n):
            tg, m = it
            if m == 0 and tg + 1 < NG:
                load(tg + 1)
            hb = self.hb[tg]
            ps, pb = self.ps[n % 4], self.pb[n % 4]
            q_, bq = qs[n % 3]
            for kc in range(8):
                self.mm(ps[:, :], wq[:, kc, 128 * m:128 * m + 128], self.hT[:, kc, tg * 512:(tg + 1) * 512],
                        kc == 0, kc == 7, [bwq0 if m == 0 else bwq, hb], [pb], signal=(kc == 7))
            self.act(q_, ps[:, :], AF.Copy, [pb], [bq], scale=(0.125 if m < 3 else 1.0))

        def p2(it, n):
            tg, m = it
            c_, bcs = cs[tg % 3]
            q_, bq = qs[n % 3]
            a_, ba = t1[n % 3]
            b_, bb = t2[n % 3]
            self.tt("dve", a_, q_, c_[:, 0, :], ALU.mult, [bq, bcs], [ba])
            for hh in range(2):
                o0 = 64 * hh
                self.tt("pool", b_[o0:o0 + 32, :], q_[o0 + 32:o0 + 64, :], c_[o0 + 32:o0 + 64, 1, :], ALU.mult,
                        [bq, bcs], [bb])
                self.tt("dve", b_[o0 + 32:o0 + 64, :], q_[o0:o0 + 32, :], c_[o0:o0 + 32, 1, :],
                        ALU.mult, [bq, bcs], [bb])

        def p3(it, n):
            tg, m = it
            d = DILS[m % 3]
            o_, bo = qkc[tg % 2]
            a_, ba = t1[n % 3]
            b_, bb = t2[n % 3]
            self.tt("dve", o_[:, m, :].rearrange("p (r n) -> p r n", r=d),
                    a_.rearrange("p (n r) -> p r n", r=d), b_.rearrange("p (n r) -> p r n", r=d),
                    ALU.add, [ba, bb], [bo])
            if m == 5:
                for m2 in range(6):
                    d2 = DILS[m2 % 3]
                    mm_ = 512 // d2
                    src_t = self.QC[m2] if m2 < 3 else self.KC[m2 - 3]
                    dst = src_t.rearrange("p (r m) -> p r m", r=d2)[:, :, tg * mm_:(tg + 1) * mm_]
                    S.dma(dst, o_[:, m2, :].rearrange("p (r n) -> p r n", r=d2), reads=[bo])

        for v_, bv in vc:
            self.memset("pool", v_, 1.0, [bv])
        vitems = []
        cnt = [0]
        for g in range(3):
            d = DILS[g]
            nb = T // (128 * d)
            for i0 in range(0, self.NB, 4):
                def vtile(g=g, d=d, nb=nb, i0=i0):
                    n0 = cnt[0]
                    v_, bv = vc[(n0 // 4) % 2]
                    for ii in range(4):
                        idx = i0 + ii
                        r, qb = idx // nb, idx % nb
                        ps, pb = self.ps[4 + (n0 + ii) % 4], self.pb[4 + (n0 + ii) % 4]
                        st = r + 128 * d * qb
                        for kc in range(8):
                            self.mm(ps[:, 0:128], self.hT[:, kc, st:st + 127 * d + 1:d],
                                    wv[:, kc, 128 * g:128 * g + 128],
                                    kc == 0, kc == 7, [bwv] + self.hb_all, [pb], signal=(kc == 7))
                        self.cp("act", v_[:, ii, :, 0:64], ps[:, 0:128].rearrange("p (j c) -> p j c", j=2), [pb], [bv])
                    cnt[0] += 4
                    S.dma(self.VC[g, i0:i0 + 4].rearrange("b p c -> p b c"),
                          v_.rearrange("p b j c -> p b (j c)"), reads=[bv])
                vitems.append(vtile)
        load(0)
        n_it, K = len(items), 3
        stages = [p1, p2, p3]
        every = max(1, n_it // max(1, len(vitems)))
        for i in range(n_it + K - 1):
            for k, f in enumerate(stages):
                j = i - k
                if 0 <= j < n_it:
                    f(items[j], j)
            if vitems and i % every == every - 1:
                vitems.pop(0)()
        while vitems:
            vitems.pop(0)()
        S.fence()
        S.collective("AllGather", ALU.bypass, self.groups, [self.KC2[:, :]], [self.KCg[:, :]], writes=[self.b_kcg])
        S.collective("AllGather", ALU.bypass, self.groups, [self.VC2[:, :]], [self.VCg[:, :]], writes=[self.b_kcg])

    def _sb_run(self, steps, e_t, L_t, w_t, cbf, extra=()):
        nc, S = self.nc, self.S
        x_t = self.A.alloc([512], F32, nbuf=2)
        bc = self.b_const
        extra = list(extra)

        def s1(st, i):
            lo, rel = st["lo"], st["rel"]
            Ab, bA = self.ps[i % 2], self.pb[i % 2]
            e_, be = e_t[i % 3]
            L_, bL = L_t[i % 4]
            self.mm(Ab[:, lo:512], st["ksl"], st["qsl"], True, True, st["bkq"], [bA])
            self.act(e_[:, lo:512], Ab[:, lo:512], AF.Exp, [bA], [be])
            self.act(L_[:, lo:512], e_[:, lo:512], AF.Ln, [be], [bL], bias=1.0)
            if rel >= 0:
                self.tt("pool", L_[:, lo:512], L_[:, lo:512], self.sbmask[:, rel, lo:512], ALU.mult, [bL, bc], [bL])

        def s2(st, i):
            lo, rel = st["lo"], st["rel"]
            L_, bL = L_t[i % 4]
            w_, bw = w_t[i % 3]
            Bb, bB = self.ps[2 + i % 2], self.pb[2 + i % 2]
            Db, bD = self.ps[6], self.pb[6]
            prev_c = st["c0"] if st["first"] else steps[i - 1]["cbuf"]
            if st["carry"]:
                self.mm(Db[:, lo:512], self.ones, L_[:, lo:512], True, True, [bc, bL], [bD])
                cb_, bcb_ = cbf[i % 4]
                st["cbuf"] = cbf[i % 4]
                if lo > 0:
                    self.memset("dve", cb_[:, 0:lo], 0.0, [bcb_])
                if prev_c is None:
                    self.cp("dve", cb_[:, lo:512], Db[:, lo:512], [bD], [bcb_])
                else:
                    self.tt("dve", cb_[:, lo:512], Db[:, lo:512], prev_c[0][:, lo:512], ALU.add,
                            [bD, prev_c[1]], [bcb_])
                if st.get("carry_out") is not None:
                    st["carry_out"](cb_, bcb_)
            if prev_c is not None:
                self.mm(Bb[:, lo:512], self.negI, prev_c[0][:, lo:512], True, False, [bc, prev_c[1]], [bB],
                        signal=False)
            self.mm(Bb[:, lo:512], self.negU, L_[:, lo:512], prev_c is None, True, [bc, bL], [bB])
            x_, bx = x_t[i % 2]
            e_, be = e_t[i % 3]
            self.act(x_[:, lo:512], Bb[:, lo:512], AF.Exp, [bB], [bx])
            self.tt("dve", w_[:, lo:512], x_[:, lo:512], e_[:, lo:512], ALU.mult, [bx, be], [bw])
            if rel >= 0:
                self.tt("pool", w_[:, lo:512], w_[:, lo:512], self.sbmask[:, rel, lo:512], ALU.mult, [bw, bc], [bw])

        def s3(st, i):
            lo = st["lo"]
            w_, bw = w_t[i % 3]
            PV, bPV = self.ps[4 + st["pv"]], self.pb[4 + st["pv"]]
            if st.get("hook") is not None:
                st["hook"]()
            if st["first"]:
                self.mm(PV[:, :], self.zeros128, self.sbmask[:, 0, :], True, False, [bc], [bPV], signal=False)
            self.mm(PV[:, lo:512], st["vsl"], w_[:, lo:512], False, st["last"], [st["bv"], bw], [bPV])
            if st["last"]:
                st["on_last"](PV, bPV)

        n = len(steps)
        every = max(1, (n - 20) // max(1, len(extra)))
        for i in range(n + 2):
            if i < n:
                s1(steps[i], i)
            if 0 <= i - 1 < n:
                s2(steps[i - 1], i - 1)
            if 0 <= i - 2 < n:
                s3(steps[i - 2], i - 2)
            if extra and i % every == every - 1:
                extra.pop(0)()
        while extra:
            extra.pop(0)()

    def attB_own(self, l):
        nc, S, A = self.nc, self.S, self.A
        self.phase(cc=False)
        NG, T, NB = self.NG, self.T, self.NB
        vb, bvb = A.alloc([NB, 384], BF16)[0]
        S.dma(vb, self.VB.rearrange("b p c -> p b c"), writes=[bvb])
        kt = A.alloc([T], BF16, nbuf=2)
        qt = A.alloc([T], BF16, nbuf=2)
        e_t = A.alloc([512], F32, nbuf=3)
        L_t = A.alloc([512], BF16, nbuf=4)
        w_t = A.alloc([512], BF16, nbuf=3)
        cbf = A.alloc([512], BF16, nbuf=4)
        yo = A.alloc([512], F32, nbuf=2)
        self.memset("pool", qt[0][0][64:128, :], 0.0, [qt[0][1]])
        self.memset("pool", qt[1][0][0:64, :], 0.0, [qt[1][1]])

        def loadk(m):
            S.dma(kt[m % 2][0], self.KB[m], writes=[kt[m % 2][1]])

        def loadq(h):
            r0 = 64 * (h % 2)
            S.dma(qt[h % 2][0][r0:r0 + 64, :], self.QB[h // 2, r0:r0 + 64, :], writes=[qt[h % 2][1]])

        steps = []
        nqg = 0
        for h in range(6):
            m, r0 = h // 2, 64 * (h % 2)
            k_, bk = kt[m % 2]
            q_, bq = qt[h % 2]
            for qg in range(NG):
                nk = 4 * qg + 4
                for kb in range(nk - 1, -1, -1):
                    rel = kb - 4 * qg
                    lo = 128 * rel if rel > 0 else 0
                    st = dict(rel=rel, lo=lo, ksl=k_[:, kb * 128:(kb + 1) * 128],
                              qsl=q_[:, qg * 512 + lo:(qg + 1) * 512], bkq=[bk, bq],
                              vsl=vb[:, kb, 128 * m:128 * m + 128], bv=bvb, first=(kb == nk - 1), last=(kb == 0),
                              pv=nqg % 2, carry=True, c0=None)
                    if qg == 0 and kb == nk - 1:
                        def hook(h=h):
                            if h + 1 < 6:
                                loadq(h + 1)
                            if h % 2 == 0 and h // 2 + 1 < 3:
                                loadk(h // 2 + 1)
                        st["hook"] = hook
                    if kb == 0:
                        def carry_out(cb_, bcb_, h=h, qg=qg):
                            S.dma(self.CRY2[h:h + 1, qg * 512:(qg + 1) * 512], cb_[0:1, :], reads=[bcb_])
                        st["carry_out"] = carry_out

                        def on_last(PV, bPV, m=m, r0=r0, qg=qg, k=nqg):
                            y_, by = yo[k % 2]
                            self.cp("act", y_[r0:r0 + 64, :], PV[r0:r0 + 64, :], [bPV], [by])
                            S.dma(self.PVO[m, r0:r0 + 64, qg * 512:(qg + 1) * 512], y_[r0:r0 + 64, :], reads=[by])
                        st["on_last"] = on_last
                    steps.append(st)
                nqg += 1
        loadk(0)
        loadq(0)
        self._sb_run(steps, e_t, L_t, w_t, cbf)
        S.fence(cc=False)
        S.collective("AllGather", ALU.bypass, self.groups, [self.CRY2[:, :]], [self.CRYg[:, :]], writes=[self.b_qg])
        S.collective("AllGather", ALU.bypass, self.groups, [self.QB2[:, :]], [self.QBg[:, :]], writes=[self.b_qg])

    def attB_ctx(self, l):
        nc, S, A = self.nc, self.S, self.A
        self.phase(cc=False)
        NG, T, NB = self.NG, self.T, self.NB
        bc = self.b_const
        vbc, bvbc = A.alloc([NB, 384], BF16)[0]
        S.dma(vbc, self.VBc.rearrange("b p c -> p b c"), reads=[self.b_kbg], writes=[bvbc])
        cand = A.alloc([T], BF16, nbuf=6)
        ksel = A.alloc([T], BF16, nbuf=2)
        qsel = A.alloc([T], BF16, nbuf=2)
        csel = A.alloc([T], BF16, nbuf=2)
        vsel = A.alloc([NB, 128], BF16, nbuf=2)
        e_t = A.alloc([512], F32, nbuf=3)
        L_t = A.alloc([512], BF16, nbuf=4)
        w_t = A.alloc([512], BF16, nbuf=3)
        cbf = A.alloc([512], BF16, nbuf=4)
        yo = A.alloc([512], F32, nbuf=2)
        cm, icm = self.ctxm[:, 0:1], self.ictx[:, 0:1]

        def blend(dst, bdst, a_, ba, b_, bb):
            self.ts("dve", dst, a_, cm, None, ALU.mult, None, [ba, bc], [bdst])
            self.stt(dst, b_, icm, dst, ALU.mult, ALU.add, [bb, bc, bdst], [bdst])

        def prep(sg):
            ha, hb_ = sg, 3 + sg
            ka, kb_, qa, qb_, ca, cb_ = cand
            S.dma(ka[0], self.KBc[ha // 2], reads=[self.b_kbg], writes=[ka[1]])
            S.dma(kb_[0], self.KBc[hb_ // 2], reads=[self.b_kbg], writes=[kb_[1]])
            blend(ksel[sg % 2][0], ksel[sg % 2][1], ka[0], ka[1], kb_[0], kb_[1])
            for (q_, bq), hh in ((qa, ha), (qb_, hb_)):
                r0 = 64 * (hh % 2)
                self.memset("pool", q_, 0.0, [bq])
                S.dma(q_[r0:r0 + 64, :], self.QBr1[hh // 2, r0:r0 + 64, :], reads=[self.b_qg], writes=[bq])
            blend(qsel[sg % 2][0], qsel[sg % 2][1], qa[0], qa[1], qb_[0], qb_[1])
            S.dma(ca[0], self.CRYg[6 + ha:7 + ha, :].partition_broadcast(128), reads=[self.b_qg], writes=[ca[1]])
            S.dma(cb_[0], self.CRYg[6 + hb_:7 + hb_, :].partition_broadcast(128), reads=[self.b_qg], writes=[cb_[1]])
            blend(csel[sg % 2][0], csel[sg % 2][1], ca[0], ca[1], cb_[0], cb_[1])
            ma, mb = ha // 2, hb_ // 2
            blend(vsel[sg % 2][0], vsel[sg % 2][1], vbc[:, :, 128 * ma:128 * ma + 128], bvbc,
                  vbc[:, :, 128 * mb:128 * mb + 128], bvbc)

        bpvc = [Buf(f"pvc{i}") for i in range(3)]
        bpvg = [Buf(f"pvg{i}") for i in range(3)]
        own = A.alloc([T], F32, parts=64, nbuf=2)
        ctxp = A.alloc([T], F32, parts=64, nbuf=2)
        ybt = A.alloc([T], BF16, parts=64, nbuf=2)

        def combine(sg):
            for h in (sg, 3 + sg):
                m, r0 = h // 2, 64 * (h % 2)
                src = (0 if h >= 3 else 1) * 128 + r0
                k = 0 if h < 3 else 1
                o_, bo = own[k]
                c_, bcx = ctxp[k]
                y_, by = ybt[k]
                S.dma(o_, self.PVO[m, r0:r0 + 64, :], writes=[bo])
                S.dma(c_, self.PVCg[sg][src:src + 64, :], reads=[bpvg[sg]], writes=[bcx])
                self.stt(y_, c_, self.ctxm[0:64, 0:1], o_, ALU.mult, ALU.add, [bcx, bo, bc], [by])
                S.dma(self.YT[2 + m, r0:r0 + 64, :], y_, reads=[by])
        steps = []
        nqg = 0
        for sg in range(3):
            k_, bk = ksel[sg % 2]
            q_, bq = qsel[sg % 2]
            c_, bcs = csel[sg % 2]
            v_, bv = vsel[sg % 2]
            for qg in range(NG):
                for kb in range(NB - 1, -1, -1):
                    st = dict(rel=-1, lo=0, ksl=k_[:, kb * 128:(kb + 1) * 128],
                              qsl=q_[:, qg * 512:(qg + 1) * 512], bkq=[bk, bq],
                              vsl=v_[:, kb, :], bv=bv, first=(kb == NB - 1), last=(kb == 0),
                              pv=nqg % 2, carry=(kb > 0), c0=(c_[:, qg * 512:(qg + 1) * 512], bcs))
                    if qg == 0 and kb == NB - 1 and sg + 1 < 3:
                        st["hook"] = (lambda sg=sg: prep(sg + 1))
                    if kb == 0:
                        def on_last(PV, bPV, sg=sg, qg=qg, k=nqg):
                            y_, by = yo[k % 2]
                            self.cp("act", y_, PV[:, :], [bPV], [by])
                            S.dma(self.PVC2[sg][:, qg * 512:(qg + 1) * 512], y_, reads=[by], writes=[bpvc[sg]])
                            if qg == NG - 1:
                                S.collective("AllGather", ALU.bypass, self.groups, [self.PVC2[sg][:, :]],
                                             [self.PVCg[sg][:, :]], reads=[bpvc[sg]], writes=[bpvg[sg]])
                            if qg == NG - 1 and sg >= 1:
                                combine(sg - 1)
                        st["on_last"] = on_last
                    steps.append(st)
                nqg += 1
        prep(0)
        extra = self.mod_items(l + 1, self.ps[7], self.pb[7]) if l + 1 < self.L else []
        self._sb_run(steps, e_t, L_t, w_t, cbf, extra)
        combine(2)

    def attC(self, l):
        nc, S, A = self.nc, self.S, self.A
        self.phase(cc=False)
        NG, T, NB = self.NG, self.T, self.NB
        bc = self.b_const
        nd = A.alloc([T], F32, parts=65, nbuf=3)
        kt = A.alloc([T], BF16, nbuf=2)
        qt = A.alloc([T], BF16, nbuf=2)
        vc = A.alloc([NB, 130], BF16, nbuf=2)
        ktc = A.alloc([T], BF16, nbuf=2)
        vcc = A.alloc([NB, 130], BF16, nbuf=2)
        p_t = A.alloc([512], BF16, nbuf=2)
        yc = A.alloc([3, 512], BF16, parts=64, nbuf=2)
        seq = [(jj, g) for jj in range(2) for g in range(3)]
        for q_, bq in qt:
            self.memset("pool", q_, 0.0, [bq])

        def load(i):
            jj, g = seq[i]
            r0 = 64 * jj
            S.dma(kt[i % 2][0], self.KC[g], writes=[kt[i % 2][1]])
            S.dma(ktc[i % 2][0], self.KCc[g], reads=[self.b_kcg], writes=[ktc[i % 2][1]])
            S.dma(vcc[i % 2][0], self.VCc[g].rearrange("b p c -> p b c"), reads=[self.b_kcg], writes=[vcc[i % 2][1]])
            vq = vcc[i % 2][0]
            self.act(vq, vq, AF.Copy, [vcc[i % 2][1], bc], [vcc[i % 2][1]], scale=self.ctxm[:, 0:1])
            if i in (3, 4):
                self.memset("pool", qt[i % 2][0][0:64, :], 0.0, [qt[i % 2][1]])
            S.dma(qt[i % 2][0][r0:r0 + 64, :], self.QC[g, r0:r0 + 64, :], writes=[qt[i % 2][1]])
            S.dma(vc[i % 2][0], self.VC[g].rearrange("b p c -> p b c"), writes=[vc[i % 2][1]])
        p_t4 = p_t + A.alloc([512], BF16, nbuf=2)
        load(0)
        NP = NB // 2
        for jj in range(2):
            items = [(g, pair) for g in range(3) for pair in range(NP)]

            def info_of(it):
                g, pair = it
                d = DILS[g]
                nb = T // (128 * d)
                res = []
                for s_ in range(2):
                    idx = 2 * pair + s_
                    qb = idx % nb
                    kidx = idx - 1 if qb > 0 else -((idx // nb) * nb + nb - 1) - 1
                    res.append((idx, qb, kidx, idx // nb))
                return d, res

            def c1(it, n):
                g, pair = it
                i = 3 * jj + g
                kT, bk = kt[i % 2]
                kC, bkc = ktc[i % 2]
                qT, bq = qt[i % 2]
                zb, bz = self.ps[n % 3], self.pb[n % 3]
                d, inf = info_of(it)
                for s_ in range(2):
                    idx, qb, kidx, r = inf[s_]
                    qsl = qT[:, idx * 128:(idx + 1) * 128]
                    self.mm(zb[:, 256 * s_:256 * s_ + 128], kT[:, idx * 128:(idx + 1) * 128], qsl, True, True,
                            [bk, bq], [bz], signal=False)
                    if kidx >= 0:
                        psl, bps_ = kT[:, kidx * 128:(kidx + 1) * 128], bk
                    else:
                        psl, bps_ = kC[:, (-kidx - 1) * 128:(-kidx) * 128], bkc
                    self.mm(zb[:, 256 * s_ + 128:256 * s_ + 256], psl, qsl, True, True,
                            [bps_, bq], [bz], signal=(s_ == 1))

            def c2(it, n):
                zb, bz = self.ps[n % 3], self.pb[n % 3]
                p_, bp = p_t4[n % 4]
                self.act(p_, zb[:, :], AF.Exp, [bz], [bp])

            def c3(it, n):
                p_, bp = p_t4[n % 4]
                d, inf = info_of(it)
                self.tt("dve", p_, p_, self.dmask.rearrange("p a b -> p (a b)"), ALU.mult, [bp, bc], [bp])

            def c4(it, n):
                g, pair = it
                i = 3 * jj + g
                v_, bv = vc[i % 2]
                vC, bvC = vcc[i % 2]
                p_, bp = p_t4[n % 4]
                ob, bo = self.ps[3 + n % 3], self.pb[3 + n % 3]
                d, inf = info_of(it)
                for s_ in range(2):
                    idx, qb, kidx, r = inf[s_]
                    self.mm(ob[0:65, 128 * s_:128 * s_ + 128], v_[:, idx, 65 * jj:65 * jj + 65],
                            p_[:, 256 * s_:256 * s_ + 128], True, False, [bv, bp], [bo], signal=False)
                    if kidx >= 0:
                        vsl, bvs = v_[:, kidx, 65 * jj:65 * jj + 65], bv
                    else:
                        vsl, bvs = vC[:, -kidx - 1, 65 * jj:65 * jj + 65], bvC
                    self.mm(ob[0:65, 128 * s_:128 * s_ + 128], vsl,
                            p_[:, 256 * s_ + 128:256 * s_ + 256], False, True, [bvs, bp], [bo], signal=(s_ == 1))

            def c5(it, n):
                g, pair = it
                i = 3 * jj + g
                if pair == 0 and i + 1 < 6:
                    load(i + 1)
                nd_, bnd = nd[g]
                ob, bo = self.ps[3 + n % 3], self.pb[3 + n % 3]
                d, inf = info_of(it)
                for s_ in range(2):
                    idx, qb, kidx, r = inf[s_]
                    st = r + 128 * d * qb
                    self.cp("act", nd_[0:65, st:st + 127 * d + 1:d], ob[0:65, 128 * s_:128 * s_ + 128],
                            [bo], [bnd])

            self.pipeline(items, [c1, c2, c3, c4, c5])
            den = nd[0][0][64:65, :]
            self.tt("dve", den, den, nd[1][0][64:65, :], ALU.add, [nd[0][1], nd[1][1]], [nd[0][1]])
            self.tt("dve", den, den, nd[2][0][64:65, :], ALU.add, [nd[0][1], nd[2][1]], [nd[0][1]])
            S.op("dve", lambda: nc.vector.reciprocal(out=den, in_=den), reads=[nd[0][1]], writes=[nd[0][1]])
            for tg in range(NG):
                bb, bbb = self.ps[6 + tg % 2], self.pb[6 + tg % 2]
                self.mm(bb[0:64, :], self.onesf[64:65, :], den[:, tg * 512:(tg + 1) * 512], True, True,
                        [bc, nd[0][1]], [bbb])
                y_, by = yc[tg % 2]
                for gg in range(3):
                    self.tt("dve", y_[:, gg, :], nd[gg][0][0:64, tg * 512:(tg + 1) * 512], bb[0:64, :], ALU.mult,
                            [nd[gg][1], bbb], [by])
                S.dma(self.YT[5:8, 64 * jj:64 * jj + 64, tg * 512:(tg + 1) * 512].rearrange("g p t -> p g t"),
                      y_, reads=[by])

    def outproj(self, l, xsrc):
        nc, S, A = self.nc, self.S, self.A
        self.phase()
        NG = self.NG
        bm = self.b_mod
        wo, bwo = A.alloc([8, 1024], BF16)[0]
        bwo0 = Buf("wo0")
        wsrc = self.w_out[l].rearrange("(c p) m -> p c m", p=128)
        S.dma(wo[:, :, 0:128], wsrc[:, :, 0:128], writes=[bwo0], q="pool")
        for q4 in range(2):
            S.dma(wo[:, 4 * q4:4 * q4 + 4, 128:1024], wsrc[:, 4 * q4:4 * q4 + 4, 128:1024], writes=[bwo], q="pool")
        yt = A.alloc([8, 512], BF16, nbuf=2)
        xt = A.alloc([8, 512], F32, nbuf=2)
        xn = A.alloc([8, 512], F32, nbuf=2)

        def load(g):
            S.dma(yt[g % 2][0], self.YT[:, :, g * 512:(g + 1) * 512].rearrange("h p t -> p h t"), writes=[yt[g % 2][1]])
            S.dma(xt[g % 2][0], xsrc[:, :, g * 512:(g + 1) * 512].rearrange("c p t -> p c t"), writes=[xt[g % 2][1]])
        load(0)
        n = 0
        for tg in range(NG):
            if tg + 1 < NG:
                load(tg + 1)
            y_, by = yt[tg % 2]
            x_, bx = xt[tg % 2]
            o_, bo = xn[tg % 2]
            for mc in range(8):
                ps, pb = self.ps[n % 4], self.pb[n % 4]
                n += 1
                for hh in range(8):
                    self.mm(ps[:, :], wo[:, hh, mc * 128:(mc + 1) * 128], y_[:, hh, :], hh == 0, hh == 7,
                            [bwo0 if mc == 0 else bwo, by], [pb], signal=(hh == 7))
                self.stt(o_[:, mc, :], ps[:, :], self.modv[:, l, 64 + mc:65 + mc], x_[:, mc, :], ALU.mult, ALU.add,
                         [pb, bm, bx], [bo])
            S.dma(self.XS[:, :, tg * 512:(tg + 1) * 512].rearrange("c p t -> p c t"), o_, reads=[bo])

    def ffn_up(self, l):
        nc, S, A = self.nc, self.S, self.A
        self.phase(keep_h=True)
        NG = self.NG
        cw, bcw = A.alloc([44, 3], F32)[0]
        cb, bcb = A.alloc([44], F32)[0]
        S.dma(cw, self.conv_wT[l], writes=[bcw])
        S.dma(cb, self.conv_bT[l], writes=[bcb])
        wsrc = self.w_up[l].rearrange("(kc p) m -> p kc m", p=128)
        bhlg = Buf("hlg")
        S.collective("AllGather", ALU.bypass, self.groups, [self.HL[:, :]], [self.HLg[:, :]], writes=[bhlg])
        hh, bhh = A.alloc([8, 2], BF16)[0]
        S.dma(hh, self.HLg[0:128, :].rearrange("p (c i) -> p c i", c=8), reads=[bhlg], writes=[bhh])
        self.act(hh, hh, AF.Copy, [bhh, self.b_const], [bhh], scale=self.ctxm[:, 0:1])
        hal = A.alloc([4, 2, 2], F32, nbuf=2)
        wg = A.alloc([8, 512], BF16, nbuf=2)
        wv = A.alloc([8, 512], BF16, nbuf=2)
        up = A.alloc([2, 514], F32, nbuf=3)
        uph = [Buf(f"uph{i}") for i in range(3)]
        acc = A.alloc([2, 512], F32, nbuf=4)
        sg = A.alloc([512], F32, nbuf=2)
        mt = A.alloc([512], BF16, nbuf=3)

        bwg0, bwv0 = Buf("wg0"), Buf("wv0")

        def loadw(jb):
            ncol = min(512, D_FF - 512 * jb)
            c0 = 0
            if jb == 0:
                S.dma(wg[0][0][:, :, 0:128], wsrc[:, :, 0:128], writes=[bwg0], q="pool")
                S.dma(wv[0][0][:, :, 0:128], wsrc[:, :, D_FF:D_FF + 128], writes=[bwv0], q="pool")
                c0 = 128
            S.dma(wg[jb % 2][0][:, :, c0:ncol], wsrc[:, :, 512 * jb + c0:512 * jb + ncol], writes=[wg[jb % 2][1]],
                  q="pool")
            S.dma(wv[jb % 2][0][:, :, c0:ncol], wsrc[:, :, D_FF + 512 * jb + c0:D_FF + 512 * jb + ncol],
                  writes=[wv[jb % 2][1]], q="pool")
        items = [(j // 4, j % 4, j, tg) for j in range(NJ) for tg in range(NG)]

        def st1(it, n):
            jb, jo, j, tg = it
            if jo == 0 and tg == 0 and jb + 1 < 6:
                loadw(jb + 1)
            g_, bg = wg[jb % 2]
            v_, bv = wv[jb % 2]
            if j == 0:
                bg, bv = bwg0, bwv0
            hb = self.hb[tg]
            G, bG = self.ps[(2 * n) % 6], self.pb[(2 * n) % 6]
            V, bV = self.ps[(2 * n + 1) % 6], self.pb[(2 * n + 1) % 6]
            hal_, bhal = hal[jb % 2]
            if jo == 0 and tg == 0:
                HB, bHB = self.ps[6 + jb % 2], self.pb[6 + jb % 2]
                for jo2 in range(4):
                    if 4 * jb + jo2 >= NJ:
                        break
                    for k2, (w_, bw_) in enumerate(((g_, wg[jb % 2][1]), (v_, wv[jb % 2][1]))):
                        c0 = (jo2 * 2 + k2) * 2
                        for kc in range(8):
                            self.mm(HB[:, c0:c0 + 2], w_[:, kc, 128 * jo2:128 * jo2 + 128], hh[:, kc, :],
                                    kc == 0, kc == 7, [bw_, bwg0, bwv0, bhh], [bHB], signal=(kc == 7))
                self.cp("act", hal_.rearrange("p a b c -> p (a b c)"), HB[:, 0:16], [bHB], [bhal])
            u_, bu = up[n % 3]
            buh = uph[n % 3]
            nu_, _ = up[(n + 1) % 3]
            nbuh = uph[(n + 1) % 3]
            a_, ba = acc[n % 4]
            rhs = self.hT[:, :, tg * 512:(tg + 1) * 512]
            for kc in range(8):
                self.mm(G[:, :], g_[:, kc, 128 * jo:128 * jo + 128], rhs[:, kc, :], kc == 0, kc == 7,
                        [bg, hb], [bG], signal=(kc == 7))
            for kc in range(8):
                self.mm(V[:, :], v_[:, kc, 128 * jo:128 * jo + 128], rhs[:, kc, :], kc == 0, kc == 7,
                        [bv, hb], [bV], signal=(kc == 7))
            if tg == 0:
                self.cp("act", u_[:, :, 0:2], hal_[:, jo, :, :], [bhal], [buh])
            self.cp("act", u_[:, 0, 2:514], G[:, :], [bG], [bu])
            self.cp("act", u_[:, 1, 2:514], V[:, :], [bV], [bu])
            if tg + 1 < NG:
                self.cp("act", nu_[:, :, 0:2], u_[:, :, 512:514], [bu], [nbuh])
            for k2, (PSb, bPS) in enumerate(((G, bG), (V, bV))):
                jj = j + NJ * k2
                self.act(a_[:, k2, :], PSb[:, :], AF.Identity, [bPS, bcw, bcb], [ba],
                         scale=cw[:, jj, 2:3], bias=cb[:, jj:jj + 1])

        def st2(it, n):
            jb, jo, j, tg = it
            u_, bu = up[n % 3]
            buh = uph[n % 3]
            a_, ba = acc[n % 4]
            for k2 in range(2):
                jj = j + NJ * k2
                self.stt(a_[:, k2, :], u_[:, k2, 1:513], cw[:, jj, 1:2], a_[:, k2, :], ALU.mult, ALU.add,
                         [bu, buh, bcw, ba], [ba])
                self.stt(a_[:, k2, :], u_[:, k2, 0:512], cw[:, jj, 0:1], a_[:, k2, :], ALU.mult, ALU.add,
                         [bu, buh, bcw, ba], [ba])

        def st3(it, n):
            jb, jo, j, tg = it
            a_, ba = acc[n % 4]
            s_, bs = sg[n % 2]
            m_, bmm = mt[n % 3]
            self.act(s_, a_[:, 0, :], AF.Silu, [ba], [bs])
            self.tt("pool", m_, s_, a_[:, 1, :], ALU.mult, [bs, ba], [bmm])
            S.dma(self.MT[j, :, tg * 512:(tg + 1) * 512], m_, reads=[bmm])

        loadw(0)
        self.pipeline(items, [st1, st2, st3])

    def ffn_down(self, l):
        nc, S, A = self.nc, self.S, self.A
        self.phase()
        NG = self.NG
        bm = self.b_mod
        wd, bwd = A.alloc([NJ, 1024], BF16)[0]
        bwd0 = Buf("wd0")
        wsrc = self.w_down[l].rearrange("(j p) m -> p j m", p=128)
        S.dma(wd[:, :, 0:128], wsrc[:, :, 0:128], writes=[bwd0], q="pool")
        for j0 in range(0, NJ, 6):
            j1 = min(NJ, j0 + 6)
            S.dma(wd[:, j0:j1, 128:1024], wsrc[:, j0:j1, 128:1024], writes=[bwd], q="pool")
        mt = A.alloc([NJ, 512], BF16, nbuf=2)
        xt = A.alloc([8, 512], F32, nbuf=2)
        xn = A.alloc([8, 512], F32, nbuf=2)

        def load(g):
            S.dma(mt[g % 2][0], self.MT[:, :, g * 512:(g + 1) * 512].rearrange("j p t -> p j t"), writes=[mt[g % 2][1]])
            S.dma(xt[g % 2][0], self.XS[:, :, g * 512:(g + 1) * 512].rearrange("c p t -> p c t"), writes=[xt[g % 2][1]])
        load(0)
        n = 0
        for tg in range(NG):
            if tg + 1 < NG:
                load(tg + 1)
            m_, bmt = mt[tg % 2]
            x_, bx = xt[tg % 2]
            o_, bo = xn[tg % 2]
            for mc in range(8):
                ps, pb = self.ps[n % 4], self.pb[n % 4]
                n += 1
                for j in range(NJ):
                    self.mm(ps[:, :], wd[:, j, mc * 128:(mc + 1) * 128], m_[:, j, :], j == 0, j == NJ - 1,
                            [bwd0 if mc == 0 else bwd, bmt], [pb], signal=(j == NJ - 1))
                self.stt(o_[:, mc, :], ps[:, :], self.modv[:, l, 72 + mc:73 + mc], x_[:, mc, :], ALU.mult, ALU.add,
                         [pb, bm, bx], [bo])
            S.dma(self.XS[:, :, tg * 512:(tg + 1) * 512].rearrange("c p t -> p c t"), o_, reads=[bo])


def _consts():
    j = np.arange(128)[:, None]
    i128 = np.arange(128)[None, :]
    ones = np.ones((128, 128), np.float32)
    negU = np.where(j >= i128, -1.0, 0.0).astype(np.float32)
    i512 = np.arange(512)[None, :]
    sb = np.stack([(r * 128 + j < i512).astype(np.float32) for r in range(4)], axis=1).reshape(128, 2048)
    diag = (j <= i128).astype(np.float32)
    prev = (j >= i128).astype(np.float32)
    dm = np.concatenate([diag, prev, diag, prev], axis=1)
    zeros128 = np.zeros((128, 128), np.float32)
    negI = -np.eye(128, dtype=np.float32)
    c_bf = np.concatenate([ones, negU, sb, dm, zeros128, negI, zeros128], axis=1).astype(ml_dtypes.bfloat16)
    tri = (j <= i128).astype(np.float32)
    onesf = np.ones((128, 64), np.float32)
    inv = (np.float32(10000.0) ** (-(np.arange(0, 64, 2, dtype=np.float32)) / np.float32(64))).astype(np.float32)
    invf = np.concatenate([inv, inv, inv, inv]).reshape(128, 1).astype(np.float32)
    sgn = np.ones((128, 1), np.float32)
    sgn[32:64] = -1.0
    sgn[96:128] = -1.0
    pad = np.zeros((128, 2), np.float32)
    c_f32 = np.concatenate([tri, onesf, invf, sgn, pad], axis=1).astype(np.float32)
    return np.ascontiguousarray(c_bf), np.ascontiguousarray(c_f32)


def _layout_shared(L, w_ada, b_ada, g_mix, w_in, g_sgu, w_sp, b_sp, w_out, g_ffn, w_up, conv_w, conv_b,
                   w_down, g_final):
    f = lambda a: np.ascontiguousarray(np.asarray(a, dtype=np.float32))
    c_bf, c_f32 = _consts()
    return {
        "w_ada": f(w_ada[:L]),
        "b_adaT": f(np.asarray(b_ada[:L]).reshape(L, 48, 128).transpose(0, 2, 1)),
        "g_mixT": f(np.asarray(g_mix[:L]).reshape(L, 8, 128).transpose(0, 2, 1)),
        "g_ffnT": f(np.asarray(g_ffn[:L]).reshape(L, 8, 128).transpose(0, 2, 1)),
        "g_finalT": f(np.asarray(g_final).reshape(8, 128).T),
        "w_in": f(w_in[:L]),
        "g_sgu": f(np.asarray(g_sgu[:L]).reshape(L, 1, 256)),
        "w_spT": f(np.asarray(w_sp[:L]).transpose(0, 1, 3, 2)),
        "b_sp": f(np.asarray(b_sp[:L]).reshape(L, 1, 512)),
        "w_out": f(w_out[:L]),
        "w_up": f(w_up[:L]),
        "conv_wT": f(np.asarray(conv_w[:L]).reshape(L, 3, 44, 128).transpose(0, 3, 2, 1)),
        "conv_bT": f(np.asarray(conv_b[:L]).reshape(L, 44, 128).transpose(0, 2, 1)),
        "w_down": f(w_down[:L]),
        "c_bf": c_bf,
        "c_f32": c_f32,
    }


def _layout_core(x_b, c_b, pos_b, T, half):
    sl = slice(half * T, (half + 1) * T)
    return {
        "xT": np.ascontiguousarray(np.asarray(x_b[sl], np.float32).T.reshape(8, 128, T)),
        "cT": np.ascontiguousarray(np.asarray(c_b, np.float32).reshape(8, 128).T),
        "pos": np.ascontiguousarray(np.asarray(pos_b[sl], np.int32)[None, :]),
        "ctxm": np.full((128, 1), float(half), np.float32),
    }


_CACHE = {}


def _get_nc(T, L, debug=False, ncores=8):
    key = (T, L, tuple(debug) if debug else None, ncores)
    if key not in _CACHE:
        b = Builder(T, L, debug=debug, ncores=ncores)
        b.build()
        print(f"[kernel] built T={T} L={L}: {b.S.nins} instructions, {b.S.nwaits} waits", flush=True)
        _CACHE[key] = b.nc
    return _CACHE[key]


def run(x, c, positions, weights, T, L, batches, debug=False):
    ncores = 2 * len(batches)
    nc = _get_nc(T, L, debug, ncores)
    shared = _layout_shared(L, **weights)
    in_maps = []
    for b in batches:
        for half in range(2):
            m = dict(shared)
            m.update(_layout_core(x[b], c[b], positions[b], T, half))
            in_maps.append(m)
    res = run_bass_kernel_spmd(nc, in_maps, core_ids=list(range(ncores)))
    return res


def kernel(x, c, positions, w_ada, b_ada, g_mix, w_in, g_sgu, w_sp, b_sp, w_out, g_ffn, w_up, conv_w,
           conv_b, w_down, g_final):
    x = np.asarray(x)
    B, T, _ = x.shape
    L = np.asarray(w_ada).shape[0]
    weights = dict(w_ada=w_ada, b_ada=b_ada, g_mix=g_mix, w_in=w_in, g_sgu=g_sgu, w_sp=w_sp, b_sp=b_sp,
                   w_out=w_out, g_ffn=g_ffn, w_up=w_up, conv_w=conv_w, conv_b=conv_b, w_down=w_down,
                   g_final=g_final)
    TH = T // 2
    res = run(x, np.asarray(c), np.asarray(positions), weights, TH, L, list(range(B)))
    out = np.empty((B, T, D), np.float32)
    for b in range(B):
        for half in range(2):
            out[b, half * TH:(half + 1) * TH] = res.results[2 * b + half]["outT"].reshape(D, TH).T
    return out
```

```python
import numpy as np
import ml_dtypes
from contextlib import ExitStack
import concourse.bass as bass
import concourse.mybir as mybir
from concourse.bass_utils import run_bass_kernel_spmd

F32 = mybir.dt.float32
BF16 = mybir.dt.bfloat16
I32 = mybir.dt.int32
AF = mybir.ActivationFunctionType
ALU = mybir.AluOpType
AX = mybir.AxisListType

D = 1024
KC = 8
IN_W = 2816
D_FF = 2816
NJ = 22
EPS = 1e-6
DILS = (1, 4, 16)


class Buf:
    __slots__ = ("name", "w", "r")

    def __init__(self, name):
        self.name = name
        self.w = None
        self.r = []


class Sched:
    ENG = ("pe", "act", "dve", "pool")

    def __init__(self, nc, es, W=16000, R=5, DP=16):
        self.nc = nc
        self.es = es
        self.eng = {"pe": nc.tensor, "act": nc.scalar, "dve": nc.vector, "pool": nc.gpsimd,
                    "sp": nc.sync}
        self.W, self.R, self.DP = W, R, DP
        self.esem = {e: [es.enter_context(nc.semaphore(f"s_{e}{i}")) for i in range(R)]
                     for e in self.ENG}
        self.cnt = {e: 0 for e in self.ENG}
        self.pend = {e: 0 for e in self.ENG}
        self.dq = ("sp", "pool")
        self.dsem = {q: [es.enter_context(nc.semaphore(f"d_{q}{i}")) for i in range(DP)]
                     for q in self.dq}
        self.dcnt = {q: 0 for q in self.dq}
        streams = self.ENG + ("sp",)
        self.we = {w: {e: 0 for e in self.ENG} for w in streams}
        self.wd = {w: {} for w in streams}
        self.nwaits = 0
        self.nins = 0

    def _wait(self, w, ev):
        if ev is None:
            return
        if ev[0] == "c":
            if self.wc[w] >= ev[1]:
                return
            self.wc[w] = ev[1]
            self.eng[w].wait_ge(self.csem, ev[1])
            self.nwaits += 1
            return
        if ev[0] == "e":
            _, e, n = ev
            if e == w and w == "pe":
                return
            if self.we[w][e] >= n:
                return
            if e == w and n > self.cnt[e]:
                raise RuntimeError("self-wait on future signal")
            self.we[w][e] = n
            slot = ((n - 1) // self.W) % self.R
            val = (n - 1) % self.W + 1
            self.eng[w].wait_ge(self.esem[e][slot], val)
            self.nwaits += 1
        else:
            _, q, k = ev
            key = (q, k % self.DP)
            val = 16 * (k // self.DP + 1)
            if self.wd[w].get(key, 0) >= val:
                return
            self.wd[w][key] = val
            self.eng[w].wait_ge(self.dsem[q][k % self.DP], val)
            self.nwaits += 1

    def _deps(self, w, reads, writes):
        for b in reads:
            self._wait(w, b.w)
        for b in writes:
            self._wait(w, b.w)
            for r in b.r:
                self._wait(w, r)

    def _commit(self, ev, reads, writes):
        for b in reads:
            b.r.append(ev)
            if len(b.r) > 64:
                b.r = b.r[-48:]
        for b in writes:
            b.w = ev
            b.r = []

    def op(self, e, fn, reads=(), writes=(), signal=True):
        self._deps(e, reads, writes)
        ins = fn()
        self.nins += 1
        if signal:
            n = self.cnt[e] + 1
            self.cnt[e] = n
            slot = ((n - 1) // self.W) % self.R
            assert n <= self.W * self.R, "semaphore ring exhausted"
            ins.then_inc(self.esem[e][slot], 1)
            self.pend[e] = 0
            ev = ("e", e, n)
        else:
            assert e == "pe"
            self.pend[e] += 1
            ev = ("e", e, self.cnt[e] + 1)
        self._commit(ev, reads, writes)
        return ev

    def dma(self, out, in_, reads=(), writes=(), q="sp", **kw):
        k = self.dcnt[q]
        if k >= self.DP:
            self._wait(q, ("d", q, k - self.DP))
        self._deps(q, reads, writes)
        ins = self.eng[q].dma_start(out=out, in_=in_, **kw)
        ins.then_inc(self.dsem[q][k % self.DP], 16)
        self.dcnt[q] = k + 1
        self.nins += 1
        ev = ("d", q, k)
        self._commit(ev, reads, writes)
        return ev

    def collective(self, kind, op, groups, ins, outs, reads=(), writes=()):
        if not hasattr(self, "csem"):
            self.csem = self.es.enter_context(self.nc.semaphore("s_cc"))
            self.ccnt = 0
            self.wc = {w: 0 for w in self.ENG + ("sp",)}
        self._deps("pool", reads, writes)
        ins_ = self.nc.gpsimd.collective_compute(kind, op, replica_groups=groups, ins=ins, outs=outs)
        ins_.then_inc(self.csem, 1)
        self.ccnt += 1
        self.nins += 1
        ev = ("c", self.ccnt)
        self._commit(ev, reads, writes)
        return ev

    def fence(self, cc=True):
        assert all(v == 0 for v in self.pend.values())
        evs = [("e", e, self.cnt[e]) for e in self.ENG if self.cnt[e] > 0]
        for q in self.dq:
            k = self.dcnt[q]
            for i in range(max(0, k - self.DP), k):
                evs.append(("d", q, i))
        if cc and getattr(self, "ccnt", 0):
            evs.append(("c", self.ccnt))
        for w in self.ENG + ("sp",):
            for ev in evs:
                self._wait(w, ev)

    def finish(self, w="sp"):
        self.fence()


class Arena:
    def __init__(self, nc, es, nwords):
        self.t = es.enter_context(nc.sbuf_tensor("arena", [128, nwords], F32))
        self.nwords = nwords
        self.off = 0
        self.n = 0

    def reset(self, off=0):
        self.off = off

    def alloc(self, shape, dt, parts=128, nbuf=1):
        nel = int(np.prod(shape))
        nw = (nel * (2 if dt == BF16 else 4) + 3) // 4
        nw = (nw + 7) // 8 * 8
        res = []
        for _ in range(nbuf):
            assert self.off + nw <= self.nwords, f"arena overflow {self.off}+{nw}>{self.nwords}"
            v = self.t[0:parts, self.off:self.off + nw]
            if dt == BF16:
                v = v.bitcast(BF16)
            elif dt == I32:
                v = v.bitcast(I32)
            v = v[:, 0:nel]
            if len(shape) == 2:
                v = v.rearrange("p (a b) -> p a b", a=shape[0])
            elif len(shape) == 3:
                v = v.rearrange("p (a b c) -> p a b c", a=shape[0], b=shape[1])
            self.n += 1
            res.append((v, Buf(f"a{self.n}")))
            self.off += nw
        return res


class Builder:
    def __init__(self, T, L, debug=False, arena_kib=176, ncores=8):
        self.T, self.L, self.debug = T, L, debug
        self.groups = [[2 * i, 2 * i + 1] for i in range(ncores // 2)]
        self.NG = T // 512
        self.NB = T // 128
        assert T % 2048 == 0
        self.nc = bass.Bass("TRN2", target_bir_lowering=False)
        self.arena_words = arena_kib * 256

    def _dram_in(self, name, shape, dt=F32):
        return self.nc.dram_tensor(name, list(shape), dt, kind="ExternalInput").ap()

    def _dram_scr(self, name, shape, dt):
        kind = "ExternalOutput" if (self.debug and name in self.debug) else "Internal"
        return self.nc.dram_tensor(name, list(shape), dt, kind=kind).ap()

    def declare(self):
        T, L = self.T, self.L
        i = self._dram_in
        self.xT = i("xT", [8, 128, T])
        self.cT = i("cT", [128, 8])
        self.pos = i("pos", [1, T], I32)
        self.w_ada = i("w_ada", [L, 1024, 6144])
        self.b_adaT = i("b_adaT", [L, 128, 48])
        self.g_mixT = i("g_mixT", [L, 128, 8])
        self.g_ffnT = i("g_ffnT", [L, 128, 8])
        self.g_finalT = i("g_finalT", [128, 8])
        self.w_in = i("w_in", [L, 1024, IN_W])
        self.g_sgu = i("g_sgu", [L, 1, 256])
        self.w_spT = i("w_spT", [L, 4, 128, 128])
        self.b_sp = i("b_sp", [L, 1, 512])
        self.w_out = i("w_out", [L, 1024, 1024])
        self.w_up = i("w_up", [L, 1024, 2 * D_FF])
        self.conv_wT = i("conv_wT", [L, 128, 44, 3])
        self.conv_bT = i("conv_bT", [L, 128, 44])
        self.w_down = i("w_down", [L, D_FF, 1024])
        self.c_bf = i("c_bf", [128, 128 + 128 + 2048 + 512 + 128 + 128 + 128], BF16)
        self.c_f32 = i("c_f32", [128, 128 + 64 + 4])
        self.ctxm_d = i("ctxm", [128, 1])
        s = self._dram_scr
        self.outT = self.nc.dram_tensor("outT", [8, 128, T], F32, kind="ExternalOutput").ap()
        self.XS = s("XS", [8, 128, T], F32)
        self.YT = s("YT", [8, 128, T], BF16)
        NB = self.NB
        self.QB2 = s("QB2", [384, T], BF16)
        self.QBg = s("QBg", [768, T], BF16)
        self.QB = self.QB2.rearrange("(m p) t -> m p t", p=128)
        self.QBr1 = self.QBg[384:768, :].rearrange("(m p) t -> m p t", p=128)
        self.CRY2 = s("CRY2", [6, T], BF16)
        self.CRYg = s("CRYg", [12, T], BF16)
        self.PVO = s("PVO", [3, 128, T], F32)
        self.PVC2 = [s(f"PVC2_{i}", [128, T], F32) for i in range(3)]
        self.PVCg = [s(f"PVCg_{i}", [256, T], F32) for i in range(3)]
        self.KB2 = s("KB2", [384, T], BF16)
        self.KBg = s("KBg", [768, T], BF16)
        self.VB2 = s("VB2", [NB * 128, 384], BF16)
        self.VBg = s("VBg", [2 * NB * 128, 384], BF16)
        self.QC = s("QC", [3, 128, T], BF16)
        self.KC2 = s("KC2", [384, T], BF16)
        self.KCg = s("KCg", [768, T], BF16)
        self.VC2 = s("VC2", [3 * NB * 128, 130], BF16)
        self.VCg = s("VCg", [6 * NB * 128, 130], BF16)
        self.HL = s("HL", [128, 16], BF16)
        self.HLg = s("HLg", [256, 16], BF16)
        self.KB = self.KB2.rearrange("(m p) t -> m p t", p=128)
        self.KBc = self.KBg[0:384, :].rearrange("(m p) t -> m p t", p=128)
        self.VB = self.VB2.rearrange("(b p) c -> b p c", p=128)
        self.VBc = self.VBg[0:NB * 128, :].rearrange("(b p) c -> b p c", p=128)
        self.KC = self.KC2.rearrange("(m p) t -> m p t", p=128)
        self.KCc = self.KCg[0:384, :].rearrange("(m p) t -> m p t", p=128)
        self.VC = self.VC2.rearrange("(g b p) c -> g b p c", g=3, p=128)
        self.VCc = self.VCg[0:3 * NB * 128, :].rearrange("(g b p) c -> g b p c", g=3, p=128)
        self.MT = s("MT", [NJ, 128, T], BF16)
        self.CSd = s("CSd", [128, 2, T], F32)

    def mm(self, out, lhsT, rhs, start, stop, reads, writes, signal=True):
        nc = self.nc
        return self.S.op("pe", lambda: nc.tensor.matmul(out, lhsT=lhsT, rhs=rhs, start=start, stop=stop),
                         reads=reads, writes=writes, signal=signal)

    def act(self, out, in_, func, reads, writes, **kw):
        nc = self.nc
        return self.S.op("act", lambda: nc.scalar.activation(out=out, in_=in_, func=func, **kw),
                         reads=reads, writes=writes)

    def tt(self, e, out, in0, in1, op, reads, writes):
        eng = self.S.eng[e]
        return self.S.op(e, lambda: eng.tensor_tensor(out=out, in0=in0, in1=in1, op=op), reads=reads, writes=writes)

    def ts(self, e, out, in0, s1, s2, op0, op1, reads, writes):
        eng = self.S.eng[e]
        if s2 is None:
            return self.S.op(e, lambda: eng.tensor_scalar(out=out, in0=in0, scalar1=s1, scalar2=None, op0=op0),
                             reads=reads, writes=writes)
        return self.S.op(e, lambda: eng.tensor_scalar(out=out, in0=in0, scalar1=s1, scalar2=s2, op0=op0, op1=op1),
                         reads=reads, writes=writes)

    def stt(self, out, in0, scalar, in1, op0, op1, reads, writes):
        nc = self.nc
        return self.S.op("dve", lambda: nc.vector.scalar_tensor_tensor(out=out, in0=in0, scalar=scalar, in1=in1,
                                                                       op0=op0, op1=op1), reads=reads, writes=writes)

    def cp(self, e, out, in_, reads, writes):
        eng = self.S.eng[e]
        if e == "act":
            return self.act(out, in_, AF.Copy, reads, writes)
        return self.S.op(e, lambda: eng.tensor_copy(out=out, in_=in_), reads=reads, writes=writes)

    def memset(self, e, out, val, writes):
        eng = self.S.eng[e]
        return self.S.op(e, lambda: eng.memset(out, val), writes=writes)

    @staticmethod
    def pipeline(items, stages):
        n, K = len(items), len(stages)
        for i in range(n + K - 1):
            for k, f in enumerate(stages):
                j = i - k
                if 0 <= j < n:
                    f(items[j], j)

    def phase(self, keep_h=False, cc=True):
        self.S.fence(cc=cc)
        self.A.reset(self.h_words if keep_h else 0)

    def build(self):
        nc = self.nc
        self.declare()
        with ExitStack() as es:
            self.es = es
            self.S = Sched(nc, es)
            self.A = Arena(nc, es, self.arena_words)
            self.h_words = 8 * self.T // 2
            self.ps = []
            self.pb = []
            for k in range(8):
                self.ps.append(es.enter_context(nc.psum_tensor(f"ps{k}", [128, 512], F32)))
                self.pb.append(Buf(f"ps{k}"))
            self.cbf = es.enter_context(nc.sbuf_tensor("cbf", [128, 128 + 128 + 2048 + 512 + 128 + 128 + 128], BF16))
            self.cf = es.enter_context(nc.sbuf_tensor("cf", [128, 128 + 64 + 4], F32))
            self.cact = es.enter_context(nc.sbuf_tensor("cact", [128, 8], F32))
            self.cact_bf = es.enter_context(nc.sbuf_tensor("cact_bf", [128, 8], BF16))
            self.modv = es.enter_context(nc.sbuf_tensor("modv", [128, self.L, 80], F32))
            self.gfin = es.enter_context(nc.sbuf_tensor("gfin", [128, 8], F32))
            self.ctxm = es.enter_context(nc.sbuf_tensor("ctxm_sb", [128, 1], F32))
            self.b_qg = Buf("qg")
            self.b_pvg = Buf("pvg")
            self.ictx = es.enter_context(nc.sbuf_tensor("ictx", [128, 1], F32))
            self.b_kbg = Buf("kbg")
            self.b_kcg = Buf("kcg")
            self.b_const = Buf("const")
            self.b_mod = Buf("mod")
            c = self.cbf
            self.ones = c[:, 0:128]
            self.negU = c[:, 128:256]
            self.sbmask = c[:, 256:2304].rearrange("p (a b) -> p a b", a=4)
            self.dmask = c[:, 2304:2816].rearrange("p (a b) -> p a b", a=2)
            self.zeros128 = c[:, 2816:2944]
            self.negI = c[:, 2944:3072]
            f = self.cf
            self.tri = f[:, 0:128]
            self.onesf = f[:, 128:192]
            self.invf = f[:, 192:193]
            self.sgn = f[:, 193:194]
            self.hT = self.A.t[:, 0:self.h_words].bitcast(BF16).rearrange("p (c t) -> p c t", c=8)
            self.hb = [Buf(f"h{g}") for g in range(self.NG)]
            self.hb_all = self.hb

            self.setup()
            xsrc = self.xT
            for l in range(self.L):
                self.norm(l, 1, xsrc)
                self.sgu(l)
                self.projB(l)
                self.projC(l)
                self.attB_own(l)
                self.attC(l)
                self.attB_ctx(l)
                self.outproj(l, xsrc)
                xsrc = self.XS
                self.norm(l, 2, xsrc)
                self.ffn_up(l)
                self.ffn_down(l)
            self.norm(None, 0, xsrc)
            self.S.finish()
        return nc

    def setup(self):
        nc, S, A = self.nc, self.S, self.A
        T = self.T
        bc = self.b_const
        S.dma(self.cbf[:], self.c_bf[:, :], writes=[bc])
        S.dma(self.cf[:], self.c_f32[:, :], writes=[bc])
        S.dma(self.cact[:], self.cT[:, :], writes=[bc])
        S.dma(self.gfin[:], self.g_finalT[:, :], writes=[bc])
        S.dma(self.ctxm[:], self.ctxm_d[:, :], writes=[bc])
        self.ts("dve", self.ictx[:], self.ctxm[:], -1.0, 1.0, ALU.mult, ALU.add, [bc], [bc])
        self.act(self.cact[:], self.cact[:], AF.Silu, [bc], [bc])
        self.cp("dve", self.cact_bf[:], self.cact[:], [bc], [bc])
        self.phase()
        PI = float(np.pi)
        C1 = 6.28125
        C2 = float(2 * np.pi - 6.28125)
        posi = A.alloc([512], I32, nbuf=2)
        ang = A.alloc([512], F32, nbuf=2)
        a2 = A.alloc([512], F32, nbuf=2)
        ki = A.alloc([512], I32, nbuf=2)
        kf = A.alloc([512], F32, nbuf=2)
        cs = A.alloc([2, 512], F32, nbuf=2)
        for g in range(self.NG):
            pi_, bpi = posi[g % 2]
            an, ban = ang[g % 2]
            a2_, ba2 = a2[g % 2]
            ki_, bki = ki[g % 2]
            kf_, bkf = kf[g % 2]
            cs_, bcs = cs[g % 2]
            S.dma(pi_, self.pos[:, g * 512:(g + 1) * 512].partition_broadcast(128), writes=[bpi])
            self.cp("dve", an, pi_, [bpi], [ban])
            self.ts("dve", an, an, self.invf[:, :], None, ALU.mult, None, [ban, bc], [ban])
            for which in range(2):
                if which == 0:
                    self.ts("dve", a2_, an, PI / 2, None, ALU.add, None, [ban], [ba2])
                else:
                    self.cp("dve", a2_, an, [ban], [ba2])
                self.ts("dve", ki_, a2_, float(1 / (2 * np.pi)), None, ALU.mult, None, [ba2], [bki])
                self.cp("dve", kf_, ki_, [bki], [bkf])
                self.stt(a2_, kf_, -C1, a2_, ALU.mult, ALU.add, [bkf, ba2], [ba2])
                self.stt(a2_, kf_, -C2, a2_, ALU.mult, ALU.add, [bkf, ba2], [ba2])
                self.ts("dve", a2_, a2_, 3.1415925, -3.1415925, ALU.min, ALU.max, [ba2], [ba2])
                if which == 0:
                    self.act(cs_[:, 0, :], a2_, AF.Sin, [ba2], [bcs])
                else:
                    self.act(cs_[:, 1, :], a2_, AF.Sin, [ba2, bc], [bcs], scale=self.sgn[:, :])
            S.dma(self.CSd[:, :, g * 512:(g + 1) * 512], cs_, reads=[bcs])
        self.mod(0)

    def mod_items(self, l, ps, pb):
        nc, S, A = self.nc, self.S, self.A
        wt = A.alloc([8, 512], BF16, nbuf=2)
        bad = A.alloc([48], F32)[0]
        gm = A.alloc([16], F32)[0]
        wsrc = self.w_ada[l].rearrange("(kc p) m -> p kc m", p=128)
        items = []

        def ld_small():
            S.dma(bad[0], self.b_adaT[l], writes=[bad[1]])
            S.dma(gm[0][:, 0:8], self.g_mixT[l], writes=[gm[1]])
            S.dma(gm[0][:, 8:16], self.g_ffnT[l], writes=[gm[1]])
        items.append(ld_small)

        def mk_load(blk):
            def f():
                w_, bw = wt[blk % 2]
                S.dma(w_, wsrc[:, :, blk * 512:(blk + 1) * 512], writes=[bw], q="pool")
            return f

        def mk_mm(blk, o):
            def f():
                w_, bw = wt[blk % 2]
                oc = blk * 4 + o
                for kc in range(8):
                    self.mm(ps[:, oc:oc + 1], w_[:, kc, o * 128:(o + 1) * 128], self.cact_bf[:, kc:kc + 1],
                            kc == 0, kc == 7, [bw, self.b_const], [pb], signal=(kc == 7))
            return f
        items.append(mk_load(0))
        for blk in range(12):
            if blk + 1 < 12:
                items.append(mk_load(blk + 1))
            for o in range(4):
                items.append(mk_mm(blk, o))

        def fin():
            mv = self.modv[:, l, :]
            bm = self.b_mod
            self.tt("dve", mv[:, 0:48], ps[:, 0:48], bad[0], ALU.add, [pb, bad[1]], [bm])
            self.stt(mv[:, 48:56], mv[:, 8:16], 1.0, gm[0][:, 0:8], ALU.add, ALU.mult, [bm, gm[1]], [bm])
            self.stt(mv[:, 56:64], mv[:, 32:40], 1.0, gm[0][:, 8:16], ALU.add, ALU.mult, [bm, gm[1]], [bm])
            self.ts("dve", mv[:, 64:72], mv[:, 16:24], 1.0, None, ALU.add, None, [bm], [bm])
            self.ts("dve", mv[:, 72:80], mv[:, 40:48], 1.0, None, ALU.add, None, [bm], [bm])
        items.append(fin)
        return items

    def mod(self, l):
        self.phase()
        for f in self.mod_items(l, self.ps[0], self.pb[0]):
            f()

    def norm(self, l, which, src):
        nc, S, A = self.nc, self.S, self.A
        self.phase(keep_h=(which != 0))
        xt = A.alloc([8, 512], F32, nbuf=3)
        sq = A.alloc([8, 512], BF16, nbuf=2)
        sd = A.alloc([512], F32, nbuf=2)
        tmp = A.alloc([8, 512], F32, nbuf=2)
        ot = A.alloc([8, 512], F32, nbuf=2) if which == 0 else None
        bm = self.b_mod
        NG = self.NG

        def load(g):
            S.dma(xt[g % 3][0], src[:, :, g * 512:(g + 1) * 512].rearrange("c p t -> p c t"), writes=[xt[g % 3][1]])

        def n1(g, n):
            if g + 1 < NG:
                load(g + 1)
            x_, bx = xt[g % 3]
            s_, bs = sq[g % 2]
            ps, pb = self.ps[g % 2], self.pb[g % 2]
            self.act(s_, x_, AF.Square, [bx], [bs])
            for c in range(8):
                self.mm(ps[:, :], self.ones, s_[:, c, :], c == 0, c == 7, [bs, self.b_const], [pb], signal=(c == 7))

        def n2(g, n):
            x_, bx = xt[g % 3]
            d_, bd = sd[g % 2]
            t_, bt = tmp[g % 2]
            ps, pb = self.ps[g % 2], self.pb[g % 2]
            self.act(d_, ps[:, :], AF.Sqrt, [pb], [bd], bias=EPS, scale=1.0 / D)
            S.op("dve", lambda: nc.vector.reciprocal(out=d_, in_=d_), reads=[bd], writes=[bd])
            for c in range(8):
                self.tt("dve", t_[:, c, :], x_[:, c, :], d_, ALU.mult, [bx, bd], [bt])

        def n3(g, n):
            t_, bt = tmp[g % 2]
            for c in range(8):
                if which == 0:
                    self.act(ot[g % 2][0][:, c, :], t_[:, c, :], AF.Identity, [bt, self.b_const], [ot[g % 2][1]],
                             scale=self.gfin[:, c:c + 1])
                else:
                    gs = self.modv[:, l, 48 + 8 * (which - 1) + c:48 + 8 * (which - 1) + c + 1]
                    sh = self.modv[:, l, 24 * (which - 1) + c:24 * (which - 1) + c + 1]
                    self.act(self.hT[:, c, g * 512:(g + 1) * 512], t_[:, c, :], AF.Identity, [bt, bm], [self.hb[g]],
                             scale=gs, bias=sh)
            if which == 0:
                S.dma(self.outT[:, :, g * 512:(g + 1) * 512].rearrange("c p t -> p c t"), ot[g % 2][0],
                      reads=[ot[g % 2][1]])

        load(0)
        self.pipeline(list(range(NG)), [n1, n2, n3])
        if which == 2:
            S.dma(self.HL.rearrange("p (c i) -> p c i", c=8), self.hT[:, :, self.T - 2:self.T],
                  reads=[self.hb[NG - 1]])

    def sgu(self, l):
        nc, S, A = self.nc, self.S, self.A
        self.phase(keep_h=True)
        NG = self.NG
        bc = self.b_const
        wA, bwA = A.alloc([8, 512], BF16)[0]
        bwA0 = Buf("wA0")
        wsrcA = self.w_in[l].rearrange("(kc p) m -> p kc m", p=128)
        S.dma(wA[:, :, 0:128], wsrcA[:, :, 0:128], writes=[bwA0], q="pool")
        S.dma(wA[:, :, 128:512], wsrcA[:, :, 128:512], writes=[bwA], q="pool")
        wsf, bwsf = A.alloc([4, 128], F32)[0]
        wsp, bwsp = A.alloc([4, 128], BF16)[0]
        S.dma(wsf, self.w_spT[l].rearrange("g s t -> s g t"), writes=[bwsf])
        for g in range(4):
            self.tt("dve", wsp[:, g, :], wsf[:, g, :], self.tri, ALU.mult, [bwsf, bc], [bwsp])
        bbc, bbbc = A.alloc([4, 4, 128], F32)[0]
        for tb in range(4):
            S.dma(bbc[:, :, tb, :], self.b_sp[l].rearrange("o (g t) -> o g t", g=4).partition_broadcast(128),
                  writes=[bbbc])
        gsg, bgsg = A.alloc([256], F32)[0]
        S.dma(gsg, self.g_sgu[l].partition_broadcast(128), writes=[bgsg])
        ug = A.alloc([2, 512], F32, nbuf=3)
        vg = A.alloc([256], F32, nbuf=6)
        sqv = A.alloc([256], F32, nbuf=3)
        ss = A.alloc([4], F32, nbuf=5)
        vn = A.alloc([256], BF16, nbuf=4)
        tmp = A.alloc([512], F32, nbuf=2)
        ya = A.alloc([2, 512], BF16, nbuf=2)
        items = [(tg, tb) for tg in range(NG) for tb in range(4)]

        def a1(it, n):
            tg, tb = it
            hb = self.hb[tg]
            if tb == 0:
                u_, bu = ug[tg % 3]
                for m in range(2):
                    ps, pb = self.ps[m], self.pb[m]
                    for kc in range(8):
                        self.mm(ps[:, :], wA[:, kc, 128 * m:128 * m + 128], self.hT[:, kc, tg * 512:(tg + 1) * 512],
                                kc == 0, kc == 7, [bwA0 if m == 0 else bwA, hb], [pb], signal=(kc == 7))
                    self.act(u_[:, m, :], ps[:, :], AF.Gelu_apprx_tanh, [pb], [bu])
            ps, pb = self.ps[2 + n % 2], self.pb[2 + n % 2]
            v_, bv = vg[n % 6]
            q_, bq = sqv[n % 3]
            t0 = tg * 512 + tb * 128
            for kc in range(8):
                self.mm(ps[:, 0:256], self.hT[:, kc, t0:t0 + 128], wA[:, kc, 256:512],
                        kc == 0, kc == 7, [bwA, hb], [pb], signal=(kc == 7))
            self.act(v_, ps[:, 0:256], AF.Gelu_apprx_tanh, [pb], [bv])
            self.act(q_, v_, AF.Square, [bv], [bq])

        def a2(it, n):
            q_, bq = sqv[n % 3]
            s_, bs = ss[n % 5]
            S.op("dve", lambda: nc.vector.tensor_reduce(out=s_, in_=q_.rearrange("p (g c) -> p g c", g=4),
                                                        axis=AX.X, op=ALU.add), reads=[bq], writes=[bs])

        def a3(it, n):
            s_, bs = ss[n % 5]
            self.act(s_, s_, AF.Sqrt, [bs], [bs], bias=EPS, scale=1.0 / 64)

        def a4(it, n):
            v_, bv = vg[n % 6]
            s_, bs = ss[n % 5]
            n_, bn = vn[n % 4]
            S.op("dve", lambda: nc.vector.reciprocal(out=s_, in_=s_), reads=[bs], writes=[bs])
            for g in range(4):
                self.stt(n_[:, 64 * g:64 * g + 64], v_[:, 64 * g:64 * g + 64], s_[:, g:g + 1],
                         gsg[:, 64 * g:64 * g + 64], ALU.mult, ALU.mult, [bv, bs, bgsg], [bn])

        def a5(it, n):
            tg, tb = it
            n_, bn = vn[n % 4]
            for g in range(4):
                m = g // 2
                self.mm(self.ps[4 + g][:, tb * 128:(tb + 1) * 128], n_[:, 128 * m:128 * m + 128], wsp[:, g, :],
                        True, True, [bn, bwsp], [self.pb[4 + g]])

            if tb == 3:
                a6(it, n)

        def a6(it, n):
            tg, tb = it
            u_, bu = ug[tg % 3]
            y_, by = ya[tg % 2]
            for g in range(4):
                t_, bt = tmp[g % 2]
                m, r0 = g // 2, 64 * (g % 2)
                self.tt("dve", t_[r0:r0 + 64, :], self.ps[4 + g][r0:r0 + 64, :],
                        bbc[r0:r0 + 64, g, :, :].rearrange("p a b -> p (a b)"), ALU.add, [self.pb[4 + g], bbbc], [bt])
                self.tt("pool", y_[r0:r0 + 64, m, :], t_[r0:r0 + 64, :], u_[r0:r0 + 64, m, :], ALU.mult, [bt, bu], [by])
            S.dma(self.YT[0:2, :, tg * 512:(tg + 1) * 512].rearrange("g p t -> p g t"), y_, reads=[by])

        self.pipeline(items, [a1, a2, a3, a4, a5])

    def projB(self, l):
        nc, S, A = self.nc, self.S, self.A
        self.phase(keep_h=True)
        NG = self.NG
        wsrc = self.w_in[l].rearrange("(kc p) m -> p kc m", p=128)
        wq, bwq = A.alloc([8, 768], BF16)[0]
        wv, bwv = A.alloc([8, 384], BF16)[0]
        bwq0 = Buf("wq0")
        S.dma(wq[:, :, 0:128], wsrc[:, :, 512:640], writes=[bwq0], q="pool")
        S.dma(wq[:, :, 128:768], wsrc[:, :, 640:1280], writes=[bwq], q="pool")
        S.dma(wv, wsrc[:, :, 1280:1664], writes=[bwv], q="pool")
        qk = A.alloc([6, 512], BF16, nbuf=2)
        vt = A.alloc([4, 384], BF16, nbuf=2)
        n = 0
        for tg in range(NG):
            hb = self.hb[tg]
            q_, bq = qk[tg % 2]
            for m in range(6):
                ps, pb = self.ps[n % 4], self.pb[n % 4]
                n += 1
                for kc in range(8):
                    self.mm(ps[:, :], wq[:, kc, 128 * m:128 * m + 128], self.hT[:, kc, tg * 512:(tg + 1) * 512],
                            kc == 0, kc == 7, [bwq0 if m == 0 else bwq, hb], [pb], signal=(kc == 7))
                if m < 3:
                    self.act(q_[:, m, :], ps[:, :], AF.Copy, [pb], [bq], scale=0.125)
                else:
                    self.cp("dve", q_[:, m, :], ps[:, :], [pb], [bq])
            S.dma(self.QB[:, :, tg * 512:(tg + 1) * 512].rearrange("j p t -> p j t"), q_[:, 0:3, :], reads=[bq])
            S.dma(self.KB[:, :, tg * 512:(tg + 1) * 512].rearrange("j p t -> p j t"), q_[:, 3:6, :], reads=[bq])
            v_, bv = vt[tg % 2]
            for tb in range(4):
                ps, pb = self.ps[4 + n % 4], self.pb[4 + n % 4]
                n += 1
                t0 = tg * 512 + tb * 128
                for kc in range(8):
                    self.mm(ps[:, 0:384], self.hT[:, kc, t0:t0 + 128], wv[:, kc, :],
                            kc == 0, kc == 7, [bwv, hb], [pb], signal=(kc == 7))
                self.cp("act" if tb % 2 else "dve", v_[:, tb, :], ps[:, 0:384], [pb], [bv])
            S.dma(self.VB[4 * tg:4 * tg + 4].rearrange("b p c -> p b c"), v_, reads=[bv])
        S.fence()
        S.collective("AllGather", ALU.bypass, self.groups, [self.KB2[:, :]], [self.KBg[:, :]], writes=[self.b_kbg])
        S.collective("AllGather", ALU.bypass, self.groups, [self.VB2[:, :]], [self.VBg[:, :]], writes=[self.b_kbg])

    def projC(self, l):
        nc, S, A = self.nc, self.S, self.A
        self.phase(keep_h=True, cc=False)
        NG, T = self.NG, self.T
        wsrc = self.w_in[l].rearrange("(kc p) m -> p kc m", p=128)
        wq, bwq = A.alloc([8, 768], BF16)[0]
        wv, bwv = A.alloc([8, 384], BF16)[0]
        bwq0 = Buf("wq0")
        S.dma(wq[:, :, 0:128], wsrc[:, :, 1664:1792], writes=[bwq0], q="pool")
        S.dma(wq[:, :, 128:768], wsrc[:, :, 1792:2432], writes=[bwq], q="pool")
        S.dma(wv, wsrc[:, :, 2432:2816], writes=[bwv], q="pool")
        cs = A.alloc([2, 512], F32, nbuf=3)
        qkc = A.alloc([6, 512], BF16, nbuf=2)
        qs = A.alloc([512], F32, nbuf=3)
        t1 = A.alloc([512], F32, nbuf=3)
        t2 = A.alloc([512], F32, nbuf=3)
        vc = A.alloc([4, 2, 65], BF16, nbuf=2)

        def load(g):
            S.dma(cs[g % 3][0], self.CSd[:, :, g * 512:(g + 1) * 512], writes=[cs[g % 3][1]])
        items = [(tg, m) for tg in range(NG) for m in range(6)]

        def p1(it, n):
            tg, m = it
            if m == 0 and tg + 1 < NG:
                load(tg + 1)
            hb = self.hb[tg]
            ps, pb = self.ps[n % 4], self.pb[n % 4]
            q_, bq = qs[n % 3]
            for kc in range(8):
                self.mm(ps[:, :], wq[:, kc, 128 * m:128 * m + 128], self.hT[:, kc, tg * 512:(tg + 1) * 512],
                        kc == 0, kc == 7, [bwq0 if m == 0 else bwq, hb], [pb], signal=(kc == 7))
            self.act(q_, ps[:, :], AF.Copy, [pb], [bq], scale=(0.125 if m < 3 else 1.0))

        def p2(it, n):
            tg, m = it
            c_, bcs = cs[tg % 3]
            q_, bq = qs[n % 3]
            a_, ba = t1[n % 3]
            b_, bb = t2[n % 3]
            self.tt("dve", a_, q_, c_[:, 0, :], ALU.mult, [bq, bcs], [ba])
            for hh in range(2):
                o0 = 64 * hh
                self.tt("pool", b_[o0:o0 + 32, :], q_[o0 + 32:o0 + 64, :], c_[o0 + 32:o0 + 64, 1, :], ALU.mult,
                        [bq, bcs], [bb])
                self.tt("dve", b_[o0 + 32:o0 + 64, :], q_[o0:o0 + 32, :], c_[o0:o0 + 32, 1, :],
                        ALU.mult, [bq, bcs], [bb])

        def p3(it, n):
            tg, m = it
            d = DILS[m % 3]
            o_, bo = qkc[tg % 2]
            a_, ba = t1[n % 3]
            b_, bb = t2[n % 3]
            self.tt("dve", o_[:, m, :].rearrange("p (r n) -> p r n", r=d),
                    a_.rearrange("p (n r) -> p r n", r=d), b_.rearrange("p (n r) -> p r n", r=d),
                    ALU.add, [ba, bb], [bo])
            if m == 5:
                for m2 in range(6):
                    d2 = DILS[m2 % 3]
                    mm_ = 512 // d2
                    src_t = self.QC[m2] if m2 < 3 else self.KC[m2 - 3]
                    dst = src_t.rearrange("p (r m) -> p r m", r=d2)[:, :, tg * mm_:(tg + 1) * mm_]
                    S.dma(dst, o_[:, m2, :].rearrange("p (r n) -> p r n", r=d2), reads=[bo])

        for v_, bv in vc:
            self.memset("pool", v_, 1.0, [bv])
        vitems = []
        cnt = [0]
        for g in range(3):
            d = DILS[g]
            nb = T // (128 * d)
            for i0 in range(0, self.NB, 4):
                def vtile(g=g, d=d, nb=nb, i0=i0):
                    n0 = cnt[0]
                    v_, bv = vc[(n0 // 4) % 2]
                    for ii in range(4):
                        idx = i0 + ii
                        r, qb = idx // nb, idx % nb
                        ps, pb = self.ps[4 + (n0 + ii) % 4], self.pb[4 + (n0 + ii) % 4]
                        st = r + 128 * d * qb
                        for kc in range(8):
                            self.mm(ps[:, 0:128], self.hT[:, kc, st:st + 127 * d + 1:d],
                                    wv[:, kc, 128 * g:128 * g + 128],
                                    kc == 0, kc == 7, [bwv] + self.hb_all, [pb], signal=(kc == 7))
                        self.cp("act", v_[:, ii, :, 0:64], ps[:, 0:128].rearrange("p (j c) -> p j c", j=2), [pb], [bv])
                    cnt[0] += 4
                    S.dma(self.VC[g, i0:i0 + 4].rearrange("b p c -> p b c"),
                          v_.rearrange("p b j c -> p b (j c)"), reads=[bv])
                vitems.append(vtile)
        load(0)
        n_it, K = len(items), 3
        stages = [p1, p2, p3]
        every = max(1, n_it // max(1, len(vitems)))
        for i in range(n_it + K - 1):
            for k, f in enumerate(stages):
                j = i - k
                if 0 <= j < n_it:
                    f(items[j], j)
            if vitems and i % every == every - 1:
                vitems.pop(0)()
        while vitems:
            vitems.pop(0)()
        S.fence()
        S.collective("AllGather", ALU.bypass, self.groups, [self.KC2[:, :]], [self.KCg[:, :]], writes=[self.b_kcg])
        S.collective("AllGather", ALU.bypass, self.groups, [self.VC2[:, :]], [self.VCg[:, :]], writes=[self.b_kcg])

    def _sb_run(self, steps, e_t, L_t, w_t, cbf, extra=()):
        nc, S = self.nc, self.S
        bc = self.b_const
        extra = list(extra)

        def s1(st, i):
            lo, rel = st["lo"], st["rel"]
            Ab, bA = self.ps[i % 2], self.pb[i % 2]
            e_, be = e_t[i % 2]
            L_, bL = L_t[i % 4]
            self.mm(Ab[:, lo:512], st["ksl"], st["qsl"], True, True, st["bkq"], [bA])
            self.act(e_[:, lo:512], Ab[:, lo:512], AF.Exp, [bA], [be])
            self.act(L_[:, lo:512], e_[:, lo:512], AF.Ln, [be], [bL], bias=1.0)
            if rel >= 0:
                self.tt("pool", L_[:, lo:512], L_[:, lo:512], self.sbmask[:, rel, lo:512], ALU.mult, [bL, bc], [bL])

        def s2(st, i):
            lo, rel = st["lo"], st["rel"]
            L_, bL = L_t[i % 4]
            w_, bw = w_t[i % 3]
            Bb, bB = self.ps[2 + i % 2], self.pb[2 + i % 2]
            Db, bD = self.ps[6], self.pb[6]
            prev_c = st["c0"] if st["first"] else steps[i - 1]["cbuf"]
            if st["carry"]:
                self.mm(Db[:, lo:512], self.ones, L_[:, lo:512], True, True, [bc, bL], [bD])
                cb_, bcb_ = cbf[i % 4]
                st["cbuf"] = cbf[i % 4]
                if lo > 0:
                    self.memset("dve", cb_[:, 0:lo], 0.0, [bcb_])
                if prev_c is None:
                    self.cp("dve", cb_[:, lo:512], Db[:, lo:512], [bD], [bcb_])
                else:
                    self.tt("dve", cb_[:, lo:512], Db[:, lo:512], prev_c[0][:, lo:512], ALU.add,
                            [bD, prev_c[1]], [bcb_])
                if st.get("carry_out") is not None:
                    st["carry_out"](cb_, bcb_)
            self.mm(Bb[:, lo:512], st["ksl"], st["qsl"], True, False, st["bkq"], [bB], signal=False)
            if prev_c is not None:
                self.mm(Bb[:, lo:512], self.negI, prev_c[0][:, lo:512], False, False, [bc, prev_c[1]], [bB],
                        signal=False)
            self.mm(Bb[:, lo:512], self.negU, L_[:, lo:512], False, True, [bc, bL], [bB])
            self.act(w_[:, lo:512], Bb[:, lo:512], AF.Exp, [bB], [bw])
            if rel >= 0:
                self.tt("pool", w_[:, lo:512], w_[:, lo:512], self.sbmask[:, rel, lo:512], ALU.mult, [bw, bc], [bw])

        def s3(st, i):
            lo = st["lo"]
            w_, bw = w_t[i % 3]
            PV, bPV = self.ps[4 + st["pv"]], self.pb[4 + st["pv"]]
            if st.get("hook") is not None:
                st["hook"]()
            if st["first"]:
                self.mm(PV[:, :], self.zeros128, self.sbmask[:, 0, :], True, False, [bc], [bPV], signal=False)
            self.mm(PV[:, lo:512], st["vsl"], w_[:, lo:512], False, st["last"], [st["bv"], bw], [bPV])
            if st["last"]:
                st["on_last"](PV, bPV)

        n = len(steps)
        every = max(1, (n - 20) // max(1, len(extra)))
        for i in range(n + 2):
            if i < n:
                s1(steps[i], i)
            if 0 <= i - 1 < n:
                s2(steps[i - 1], i - 1)
            if 0 <= i - 2 < n:
                s3(steps[i - 2], i - 2)
            if extra and i % every == every - 1:
                extra.pop(0)()
        while extra:
            extra.pop(0)()

    def attB_own(self, l):
        nc, S, A = self.nc, self.S, self.A
        self.phase(cc=False)
        NG, T, NB = self.NG, self.T, self.NB
        vb, bvb = A.alloc([NB, 384], BF16)[0]
        S.dma(vb, self.VB.rearrange("b p c -> p b c"), writes=[bvb])
        kt = A.alloc([T], BF16, nbuf=2)
        qt = A.alloc([T], BF16, nbuf=2)
        e_t = A.alloc([512], F32, nbuf=2)
        L_t = A.alloc([512], BF16, nbuf=4)
        w_t = A.alloc([512], BF16, nbuf=3)
        cbf = A.alloc([512], BF16, nbuf=4)
        yo = A.alloc([512], F32, nbuf=2)
        self.memset("pool", qt[0][0][64:128, :], 0.0, [qt[0][1]])
        self.memset("pool", qt[1][0][0:64, :], 0.0, [qt[1][1]])

        def loadk(m):
            S.dma(kt[m % 2][0], self.KB[m], writes=[kt[m % 2][1]])

        def loadq(h):
            r0 = 64 * (h % 2)
            S.dma(qt[h % 2][0][r0:r0 + 64, :], self.QB[h // 2, r0:r0 + 64, :], writes=[qt[h % 2][1]])

        steps = []
        nqg = 0
        for h in range(6):
            m, r0 = h // 2, 64 * (h % 2)
            k_, bk = kt[m % 2]
            q_, bq = qt[h % 2]
            for qg in range(NG):
                nk = 4 * qg + 4
                for kb in range(nk - 1, -1, -1):
                    rel = kb - 4 * qg
                    lo = 128 * rel if rel > 0 else 0
                    st = dict(rel=rel, lo=lo, ksl=k_[:, kb * 128:(kb + 1) * 128],
                              qsl=q_[:, qg * 512 + lo:(qg + 1) * 512], bkq=[bk, bq],
                              vsl=vb[:, kb, 128 * m:128 * m + 128], bv=bvb, first=(kb == nk - 1), last=(kb == 0),
                              pv=nqg % 2, carry=True, c0=None)
                    if qg == 0 and kb == nk - 1:
                        def hook(h=h):
                            if h + 1 < 6:
                                loadq(h + 1)
                            if h % 2 == 0 and h // 2 + 1 < 3:
                                loadk(h // 2 + 1)
                        st["hook"] = hook
                    if kb == 0:
                        def carry_out(cb_, bcb_, h=h, qg=qg):
                            S.dma(self.CRY2[h:h + 1, qg * 512:(qg + 1) * 512], cb_[0:1, :], reads=[bcb_])
                        st["carry_out"] = carry_out

                        def on_last(PV, bPV, m=m, r0=r0, qg=qg, k=nqg):
                            y_, by = yo[k % 2]
                            self.cp("act", y_[r0:r0 + 64, :], PV[r0:r0 + 64, :], [bPV], [by])
                            S.dma(self.PVO[m, r0:r0 + 64, qg * 512:(qg + 1) * 512], y_[r0:r0 + 64, :], reads=[by])
                        st["on_last"] = on_last
                    steps.append(st)
                nqg += 1
        loadk(0)
        loadq(0)
        self._sb_run(steps, e_t, L_t, w_t, cbf)
        S.fence(cc=False)
        S.collective("AllGather", ALU.bypass, self.groups, [self.CRY2[:, :]], [self.CRYg[:, :]], writes=[self.b_qg])
        S.collective("AllGather", ALU.bypass, self.groups, [self.QB2[:, :]], [self.QBg[:, :]], writes=[self.b_qg])

    def attB_ctx(self, l):
        nc, S, A = self.nc, self.S, self.A
        self.phase(cc=False)
        NG, T, NB = self.NG, self.T, self.NB
        bc = self.b_const
        vbc, bvbc = A.alloc([NB, 384], BF16)[0]
        S.dma(vbc, self.VBc.rearrange("b p c -> p b c"), reads=[self.b_kbg], writes=[bvbc])
        cand = A.alloc([T], BF16, nbuf=6)
        ksel = A.alloc([T], BF16, nbuf=2)
        qsel = A.alloc([T], BF16, nbuf=2)
        csel = A.alloc([T], BF16, nbuf=2)
        vsel = A.alloc([NB, 128], BF16, nbuf=2)
        e_t = A.alloc([512], F32, nbuf=2)
        L_t = A.alloc([512], BF16, nbuf=4)
        w_t = A.alloc([512], BF16, nbuf=3)
        cbf = A.alloc([512], BF16, nbuf=4)
        yo = A.alloc([512], F32, nbuf=2)
        cm, icm = self.ctxm[:, 0:1], self.ictx[:, 0:1]

        def blend(dst, bdst, a_, ba, b_, bb):
            self.ts("dve", dst, a_, cm, None, ALU.mult, None, [ba, bc], [bdst])
            self.stt(dst, b_, icm, dst, ALU.mult, ALU.add, [bb, bc, bdst], [bdst])

        def prep(sg):
            ha, hb_ = sg, 3 + sg
            ka, kb_, qa, qb_, ca, cb_ = cand
            S.dma(ka[0], self.KBc[ha // 2], reads=[self.b_kbg], writes=[ka[1]])
            S.dma(kb_[0], self.KBc[hb_ // 2], reads=[self.b_kbg], writes=[kb_[1]])
            blend(ksel[sg % 2][0], ksel[sg % 2][1], ka[0], ka[1], kb_[0], kb_[1])
            for (q_, bq), hh in ((qa, ha), (qb_, hb_)):
                r0 = 64 * (hh % 2)
                self.memset("pool", q_, 0.0, [bq])
                S.dma(q_[r0:r0 + 64, :], self.QBr1[hh // 2, r0:r0 + 64, :], reads=[self.b_qg], writes=[bq])
            blend(qsel[sg % 2][0], qsel[sg % 2][1], qa[0], qa[1], qb_[0], qb_[1])
            S.dma(ca[0], self.CRYg[6 + ha:7 + ha, :].partition_broadcast(128), reads=[self.b_qg], writes=[ca[1]])
            S.dma(cb_[0], self.CRYg[6 + hb_:7 + hb_, :].partition_broadcast(128), reads=[self.b_qg], writes=[cb_[1]])
            blend(csel[sg % 2][0], csel[sg % 2][1], ca[0], ca[1], cb_[0], cb_[1])
            ma, mb = ha // 2, hb_ // 2
            blend(vsel[sg % 2][0], vsel[sg % 2][1], vbc[:, :, 128 * ma:128 * ma + 128], bvbc,
                  vbc[:, :, 128 * mb:128 * mb + 128], bvbc)

        bpvc = [Buf(f"pvc{i}") for i in range(3)]
        bpvg = [Buf(f"pvg{i}") for i in range(3)]
        own = A.alloc([T], F32, parts=64, nbuf=2)
        ctxp = A.alloc([T], F32, parts=64, nbuf=2)
        ybt = A.alloc([T], BF16, parts=64, nbuf=2)

        def combine(sg):
            for h in (sg, 3 + sg):
                m, r0 = h // 2, 64 * (h % 2)
                src = (0 if h >= 3 else 1) * 128 + r0
                k = 0 if h < 3 else 1
                o_, bo = own[k]
                c_, bcx = ctxp[k]
                y_, by = ybt[k]
                S.dma(o_, self.PVO[m, r0:r0 + 64, :], writes=[bo])
                S.dma(c_, self.PVCg[sg][src:src + 64, :], reads=[bpvg[sg]], writes=[bcx])
                self.stt(y_, c_, self.ctxm[0:64, 0:1], o_, ALU.mult, ALU.add, [bcx, bo, bc], [by])
                S.dma(self.YT[2 + m, r0:r0 + 64, :], y_, reads=[by])
        steps = []
        nqg = 0
        for sg in range(3):
            k_, bk = ksel[sg % 2]
            q_, bq = qsel[sg % 2]
            c_, bcs = csel[sg % 2]
            v_, bv = vsel[sg % 2]
            for qg in range(NG):
                for kb in range(NB - 1, -1, -1):
                    st = dict(rel=-1, lo=0, ksl=k_[:, kb * 128:(kb + 1) * 128],
                              qsl=q_[:, qg * 512:(qg + 1) * 512], bkq=[bk, bq],
                              vsl=v_[:, kb, :], bv=bv, first=(kb == NB - 1), last=(kb == 0),
                              pv=nqg % 2, carry=(kb > 0), c0=(c_[:, qg * 512:(qg + 1) * 512], bcs))
                    if qg == 0 and kb == NB - 1 and sg + 1 < 3:
                        st["hook"] = (lambda sg=sg: prep(sg + 1))
                    if kb == 0:
                        def on_last(PV, bPV, sg=sg, qg=qg, k=nqg):
                            y_, by = yo[k % 2]
                            self.cp("act", y_, PV[:, :], [bPV], [by])
                            S.dma(self.PVC2[sg][:, qg * 512:(qg + 1) * 512], y_, reads=[by], writes=[bpvc[sg]])
                            if qg == NG - 1:
                                S.collective("AllGather", ALU.bypass, self.groups, [self.PVC2[sg][:, :]],
                                             [self.PVCg[sg][:, :]], reads=[bpvc[sg]], writes=[bpvg[sg]])
                            if qg == NG - 1 and sg >= 1:
                                combine(sg - 1)
                        st["on_last"] = on_last
                    steps.append(st)
                nqg += 1
        prep(0)
        extra = self.mod_items(l + 1, self.ps[7], self.pb[7]) if l + 1 < self.L else []
        self._sb_run(steps, e_t, L_t, w_t, cbf, extra)
        combine(2)

    def attC(self, l):
        nc, S, A = self.nc, self.S, self.A
        self.phase(cc=False)
        NG, T, NB = self.NG, self.T, self.NB
        bc = self.b_const
        nd = A.alloc([T], F32, parts=65, nbuf=3)
        kt = A.alloc([T], BF16, nbuf=2)
        qt = A.alloc([T], BF16, nbuf=2)
        vc = A.alloc([NB, 130], BF16, nbuf=2)
        ktc = A.alloc([T], BF16, nbuf=2)
        vcc = A.alloc([NB, 130], BF16, nbuf=2)
        p_t = A.alloc([512], BF16, nbuf=2)
        yc = A.alloc([3, 512], BF16, parts=64, nbuf=2)
        seq = [(jj, g) for jj in range(2) for g in range(3)]
        for q_, bq in qt:
            self.memset("pool", q_, 0.0, [bq])

        def load(i):
            jj, g = seq[i]
            r0 = 64 * jj
            S.dma(kt[i % 2][0], self.KC[g], writes=[kt[i % 2][1]])
            S.dma(ktc[i % 2][0], self.KCc[g], reads=[self.b_kcg], writes=[ktc[i % 2][1]])
            S.dma(vcc[i % 2][0], self.VCc[g].rearrange("b p c -> p b c"), reads=[self.b_kcg], writes=[vcc[i % 2][1]])
            vq = vcc[i % 2][0]
            self.act(vq, vq, AF.Copy, [vcc[i % 2][1], bc], [vcc[i % 2][1]], scale=self.ctxm[:, 0:1])
            if i in (3, 4):
                self.memset("pool", qt[i % 2][0][0:64, :], 0.0, [qt[i % 2][1]])
            S.dma(qt[i % 2][0][r0:r0 + 64, :], self.QC[g, r0:r0 + 64, :], writes=[qt[i % 2][1]])
            S.dma(vc[i % 2][0], self.VC[g].rearrange("b p c -> p b c"), writes=[vc[i % 2][1]])
        p_t4 = p_t + A.alloc([512], BF16, nbuf=2)
        load(0)
        NP = NB // 2
        for jj in range(2):
            items = [(g, pair) for g in range(3) for pair in range(NP)]

            def info_of(it):
                g, pair = it
                d = DILS[g]
                nb = T // (128 * d)
                res = []
                for s_ in range(2):
                    idx = 2 * pair + s_
                    qb = idx % nb
                    kidx = idx - 1 if qb > 0 else -((idx // nb) * nb + nb - 1) - 1
                    res.append((idx, qb, kidx, idx // nb))
                return d, res

            def c1(it, n):
                g, pair = it
                i = 3 * jj + g
                kT, bk = kt[i % 2]
                kC, bkc = ktc[i % 2]
                qT, bq = qt[i % 2]
                zb, bz = self.ps[n % 3], self.pb[n % 3]
                d, inf = info_of(it)
                for s_ in range(2):
                    idx, qb, kidx, r = inf[s_]
                    qsl = qT[:, idx * 128:(idx + 1) * 128]
                    self.mm(zb[:, 256 * s_:256 * s_ + 128], kT[:, idx * 128:(idx + 1) * 128], qsl, True, True,
                            [bk, bq], [bz], signal=False)
                    if kidx >= 0:
                        psl, bps_ = kT[:, kidx * 128:(kidx + 1) * 128], bk
                    else:
                        psl, bps_ = kC[:, (-kidx - 1) * 128:(-kidx) * 128], bkc
                    self.mm(zb[:, 256 * s_ + 128:256 * s_ + 256], psl, qsl, True, True,
                            [bps_, bq], [bz], signal=(s_ == 1))

            def c2(it, n):
                zb, bz = self.ps[n % 3], self.pb[n % 3]
                p_, bp = p_t4[n % 4]
                self.act(p_, zb[:, :], AF.Exp, [bz], [bp])

            def c3(it, n):
                p_, bp = p_t4[n % 4]
                d, inf = info_of(it)
                self.tt("dve", p_, p_, self.dmask.rearrange("p a b -> p (a b)"), ALU.mult, [bp, bc], [bp])

            def c4(it, n):
                g, pair = it
                i = 3 * jj + g
                v_, bv = vc[i % 2]
                vC, bvC = vcc[i % 2]
                p_, bp = p_t4[n % 4]
                ob, bo = self.ps[3 + n % 3], self.pb[3 + n % 3]
                d, inf = info_of(it)
                for s_ in range(2):
                    idx, qb, kidx, r = inf[s_]
                    self.mm(ob[0:65, 128 * s_:128 * s_ + 128], v_[:, idx, 65 * jj:65 * jj + 65],
                            p_[:, 256 * s_:256 * s_ + 128], True, False, [bv, bp], [bo], signal=False)
                    if kidx >= 0:
                        vsl, bvs = v_[:, kidx, 65 * jj:65 * jj + 65], bv
                    else:
                        vsl, bvs = vC[:, -kidx - 1, 65 * jj:65 * jj + 65], bvC
                    self.mm(ob[0:65, 128 * s_:128 * s_ + 128], vsl,
                            p_[:, 256 * s_ + 128:256 * s_ + 256], False, True, [bvs, bp], [bo], signal=(s_ == 1))

            def c5(it, n):
                g, pair = it
                i = 3 * jj + g
                if pair == 0 and i + 1 < 6:
                    load(i + 1)
                nd_, bnd = nd[g]
                ob, bo = self.ps[3 + n % 3], self.pb[3 + n % 3]
                d, inf = info_of(it)
                for s_ in range(2):
                    idx, qb, kidx, r = inf[s_]
                    st = r + 128 * d * qb
                    self.cp("act" if s_ else "dve", nd_[0:65, st:st + 127 * d + 1:d], ob[0:65, 128 * s_:128 * s_ + 128],
                            [bo], [bnd])

            self.pipeline(items, [c1, c2, c3, c4, c5])
            den = nd[0][0][64:65, :]
            self.tt("dve", den, den, nd[1][0][64:65, :], ALU.add, [nd[0][1], nd[1][1]], [nd[0][1]])
            self.tt("dve", den, den, nd[2][0][64:65, :], ALU.add, [nd[0][1], nd[2][1]], [nd[0][1]])
            S.op("dve", lambda: nc.vector.reciprocal(out=den, in_=den), reads=[nd[0][1]], writes=[nd[0][1]])
            for tg in range(NG):
                bb, bbb = self.ps[6 + tg % 2], self.pb[6 + tg % 2]
                self.mm(bb[0:64, :], self.onesf[64:65, :], den[:, tg * 512:(tg + 1) * 512], True, True,
                        [bc, nd[0][1]], [bbb])
                y_, by = yc[tg % 2]
                for gg in range(3):
                    self.tt("dve", y_[:, gg, :], nd[gg][0][0:64, tg * 512:(tg + 1) * 512], bb[0:64, :], ALU.mult,
                            [nd[gg][1], bbb], [by])
                S.dma(self.YT[5:8, 64 * jj:64 * jj + 64, tg * 512:(tg + 1) * 512].rearrange("g p t -> p g t"),
                      y_, reads=[by])

    def outproj(self, l, xsrc):
        nc, S, A = self.nc, self.S, self.A
        self.phase()
        NG = self.NG
        bm = self.b_mod
        wo, bwo = A.alloc([8, 1024], BF16)[0]
        bwo0 = Buf("wo0")
        wsrc = self.w_out[l].rearrange("(c p) m -> p c m", p=128)
        S.dma(wo[:, :, 0:128], wsrc[:, :, 0:128], writes=[bwo0], q="pool")
        for q4 in range(2):
            S.dma(wo[:, 4 * q4:4 * q4 + 4, 128:1024], wsrc[:, 4 * q4:4 * q4 + 4, 128:1024], writes=[bwo], q="pool")
        yt = A.alloc([8, 512], BF16, nbuf=2)
        xt = A.alloc([8, 512], F32, nbuf=2)
        xn = A.alloc([8, 512], F32, nbuf=2)

        def load(g):
            S.dma(yt[g % 2][0], self.YT[:, :, g * 512:(g + 1) * 512].rearrange("h p t -> p h t"), writes=[yt[g % 2][1]])
            S.dma(xt[g % 2][0], xsrc[:, :, g * 512:(g + 1) * 512].rearrange("c p t -> p c t"), writes=[xt[g % 2][1]])
        load(0)
        n = 0
        for tg in range(NG):
            if tg + 1 < NG:
                load(tg + 1)
            y_, by = yt[tg % 2]
            x_, bx = xt[tg % 2]
            o_, bo = xn[tg % 2]
            for mc in range(8):
                ps, pb = self.ps[n % 4], self.pb[n % 4]
                n += 1
                for hh in range(8):
                    self.mm(ps[:, :], wo[:, hh, mc * 128:(mc + 1) * 128], y_[:, hh, :], hh == 0, hh == 7,
                            [bwo0 if mc == 0 else bwo, by], [pb], signal=(hh == 7))
                self.stt(o_[:, mc, :], ps[:, :], self.modv[:, l, 64 + mc:65 + mc], x_[:, mc, :], ALU.mult, ALU.add,
                         [pb, bm, bx], [bo])
            S.dma(self.XS[:, :, tg * 512:(tg + 1) * 512].rearrange("c p t -> p c t"), o_, reads=[bo])

    def ffn_up(self, l):
        nc, S, A = self.nc, self.S, self.A
        self.phase(keep_h=True)
        NG = self.NG
        cw, bcw = A.alloc([44, 3], F32)[0]
        cb, bcb = A.alloc([44], F32)[0]
        S.dma(cw, self.conv_wT[l], writes=[bcw])
        S.dma(cb, self.conv_bT[l], writes=[bcb])
        wsrc = self.w_up[l].rearrange("(kc p) m -> p kc m", p=128)
        bhlg = Buf("hlg")
        S.collective("AllGather", ALU.bypass, self.groups, [self.HL[:, :]], [self.HLg[:, :]], writes=[bhlg])
        hh, bhh = A.alloc([8, 2], BF16)[0]
        S.dma(hh, self.HLg[0:128, :].rearrange("p (c i) -> p c i", c=8), reads=[bhlg], writes=[bhh])
        self.act(hh, hh, AF.Copy, [bhh, self.b_const], [bhh], scale=self.ctxm[:, 0:1])
        hal = A.alloc([4, 2, 2], F32, nbuf=2)
        wg = A.alloc([8, 512], BF16, nbuf=2)
        wv = A.alloc([8, 512], BF16, nbuf=2)
        up = A.alloc([2, 514], F32, nbuf=3)
        uph = [Buf(f"uph{i}") for i in range(3)]
        acc = A.alloc([2, 512], F32, nbuf=4)
        sg = A.alloc([512], F32, nbuf=2)
        mt = A.alloc([512], BF16, nbuf=3)

        bwg0, bwv0 = Buf("wg0"), Buf("wv0")

        def loadw(jb):
            ncol = min(512, D_FF - 512 * jb)
            c0 = 0
            if jb == 0:
                S.dma(wg[0][0][:, :, 0:128], wsrc[:, :, 0:128], writes=[bwg0], q="pool")
                S.dma(wv[0][0][:, :, 0:128], wsrc[:, :, D_FF:D_FF + 128], writes=[bwv0], q="pool")
                c0 = 128
            S.dma(wg[jb % 2][0][:, :, c0:ncol], wsrc[:, :, 512 * jb + c0:512 * jb + ncol], writes=[wg[jb % 2][1]],
                  q="pool")
            S.dma(wv[jb % 2][0][:, :, c0:ncol], wsrc[:, :, D_FF + 512 * jb + c0:D_FF + 512 * jb + ncol],
                  writes=[wv[jb % 2][1]], q="pool")
        items = [(j // 4, j % 4, j, tg) for j in range(NJ) for tg in range(NG)]

        def st1(it, n):
            jb, jo, j, tg = it
            if jo == 0 and tg == 0 and jb + 1 < 6:
                loadw(jb + 1)
            g_, bg = wg[jb % 2]
            v_, bv = wv[jb % 2]
            if j == 0:
                bg, bv = bwg0, bwv0
            hb = self.hb[tg]
            G, bG = self.ps[(2 * n) % 6], self.pb[(2 * n) % 6]
            V, bV = self.ps[(2 * n + 1) % 6], self.pb[(2 * n + 1) % 6]
            hal_, bhal = hal[jb % 2]
            if jo == 0 and tg == 0:
                HB, bHB = self.ps[6 + jb % 2], self.pb[6 + jb % 2]
                for jo2 in range(4):
                    if 4 * jb + jo2 >= NJ:
                        break
                    for k2, (w_, bw_) in enumerate(((g_, wg[jb % 2][1]), (v_, wv[jb % 2][1]))):
                        c0 = (jo2 * 2 + k2) * 2
                        for kc in range(8):
                            self.mm(HB[:, c0:c0 + 2], w_[:, kc, 128 * jo2:128 * jo2 + 128], hh[:, kc, :],
                                    kc == 0, kc == 7, [bw_, bwg0, bwv0, bhh], [bHB], signal=(kc == 7))
                self.cp("act", hal_.rearrange("p a b c -> p (a b c)"), HB[:, 0:16], [bHB], [bhal])
            u_, bu = up[n % 3]
            buh = uph[n % 3]
            nu_, _ = up[(n + 1) % 3]
            nbuh = uph[(n + 1) % 3]
            a_, ba = acc[n % 4]
            rhs = self.hT[:, :, tg * 512:(tg + 1) * 512]
            for kc in range(8):
                self.mm(G[:, :], g_[:, kc, 128 * jo:128 * jo + 128], rhs[:, kc, :], kc == 0, kc == 7,
                        [bg, hb], [bG], signal=(kc == 7))
            for kc in range(8):
                self.mm(V[:, :], v_[:, kc, 128 * jo:128 * jo + 128], rhs[:, kc, :], kc == 0, kc == 7,
                        [bv, hb], [bV], signal=(kc == 7))
            if tg == 0:
                self.cp("act", u_[:, :, 0:2], hal_[:, jo, :, :], [bhal], [buh])
            self.cp("act", u_[:, 0, 2:514], G[:, :], [bG], [bu])
            self.cp("act", u_[:, 1, 2:514], V[:, :], [bV], [bu])
            if tg + 1 < NG:
                self.cp("act", nu_[:, :, 0:2], u_[:, :, 512:514], [bu], [nbuh])
            for k2, (PSb, bPS) in enumerate(((G, bG), (V, bV))):
                jj = j + NJ * k2
                self.act(a_[:, k2, :], PSb[:, :], AF.Identity, [bPS, bcw, bcb], [ba],
                         scale=cw[:, jj, 2:3], bias=cb[:, jj:jj + 1])

        def st2(it, n):
            jb, jo, j, tg = it
            u_, bu = up[n % 3]
            buh = uph[n % 3]
            a_, ba = acc[n % 4]
            for k2 in range(2):
                jj = j + NJ * k2
                self.stt(a_[:, k2, :], u_[:, k2, 1:513], cw[:, jj, 1:2], a_[:, k2, :], ALU.mult, ALU.add,
                         [bu, buh, bcw, ba], [ba])
                self.stt(a_[:, k2, :], u_[:, k2, 0:512], cw[:, jj, 0:1], a_[:, k2, :], ALU.mult, ALU.add,
                         [bu, buh, bcw, ba], [ba])

        def st3(it, n):
            jb, jo, j, tg = it
            a_, ba = acc[n % 4]
            s_, bs = sg[n % 2]
            m_, bmm = mt[n % 3]
            self.act(s_, a_[:, 0, :], AF.Silu, [ba], [bs])
            self.tt("pool", m_, s_, a_[:, 1, :], ALU.mult, [bs, ba], [bmm])
            S.dma(self.MT[j, :, tg * 512:(tg + 1) * 512], m_, reads=[bmm])

        loadw(0)
        self.pipeline(items, [st1, st2, st3])

    def ffn_down(self, l):
        nc, S, A = self.nc, self.S, self.A
        self.phase()
        NG = self.NG
        bm = self.b_mod
        wd, bwd = A.alloc([NJ, 1024], BF16)[0]
        bwd0 = Buf("wd0")
        wsrc = self.w_down[l].rearrange("(j p) m -> p j m", p=128)
        S.dma(wd[:, :, 0:128], wsrc[:, :, 0:128], writes=[bwd0], q="pool")
        for j0 in range(0, NJ, 6):
            j1 = min(NJ, j0 + 6)
            S.dma(wd[:, j0:j1, 128:1024], wsrc[:, j0:j1, 128:1024], writes=[bwd], q="pool")
        mt = A.alloc([NJ, 512], BF16, nbuf=2)
        xt = A.alloc([8, 512], F32, nbuf=2)
        xn = A.alloc([8, 512], F32, nbuf=2)

        def load(g):
            S.dma(mt[g % 2][0], self.MT[:, :, g * 512:(g + 1) * 512].rearrange("j p t -> p j t"), writes=[mt[g % 2][1]])
            S.dma(xt[g % 2][0], self.XS[:, :, g * 512:(g + 1) * 512].rearrange("c p t -> p c t"), writes=[xt[g % 2][1]])
        load(0)
        n = 0
        for tg in range(NG):
            if tg + 1 < NG:
                load(tg + 1)
            m_, bmt = mt[tg % 2]
            x_, bx = xt[tg % 2]
            o_, bo = xn[tg % 2]
            for mc in range(8):
                ps, pb = self.ps[n % 4], self.pb[n % 4]
                n += 1
                for j in range(NJ):
                    self.mm(ps[:, :], wd[:, j, mc * 128:(mc + 1) * 128], m_[:, j, :], j == 0, j == NJ - 1,
                            [bwd0 if mc == 0 else bwd, bmt], [pb], signal=(j == NJ - 1))
                self.stt(o_[:, mc, :], ps[:, :], self.modv[:, l, 72 + mc:73 + mc], x_[:, mc, :], ALU.mult, ALU.add,
                         [pb, bm, bx], [bo])
            S.dma(self.XS[:, :, tg * 512:(tg + 1) * 512].rearrange("c p t -> p c t"), o_, reads=[bo])


def _consts():
    j = np.arange(128)[:, None]
    i128 = np.arange(128)[None, :]
    ones = np.ones((128, 128), np.float32)
    negU = np.where(j >= i128, -1.0, 0.0).astype(np.float32)
    i512 = np.arange(512)[None, :]
    sb = np.stack([(r * 128 + j < i512).astype(np.float32) for r in range(4)], axis=1).reshape(128, 2048)
    diag = (j <= i128).astype(np.float32)
    prev = (j >= i128).astype(np.float32)
    dm = np.concatenate([diag, prev, diag, prev], axis=1)
    zeros128 = np.zeros((128, 128), np.float32)
    negI = -np.eye(128, dtype=np.float32)
    c_bf = np.concatenate([ones, negU, sb, dm, zeros128, negI, zeros128], axis=1).astype(ml_dtypes.bfloat16)
    tri = (j <= i128).astype(np.float32)
    onesf = np.ones((128, 64), np.float32)
    inv = (np.float32(10000.0) ** (-(np.arange(0, 64, 2, dtype=np.float32)) / np.float32(64))).astype(np.float32)
    invf = np.concatenate([inv, inv, inv, inv]).reshape(128, 1).astype(np.float32)
    sgn = np.ones((128, 1), np.float32)
    sgn[32:64] = -1.0
    sgn[96:128] = -1.0
    pad = np.zeros((128, 2), np.float32)
    c_f32 = np.concatenate([tri, onesf, invf, sgn, pad], axis=1).astype(np.float32)
    return np.ascontiguousarray(c_bf), np.ascontiguousarray(c_f32)


def _layout_shared(L, w_ada, b_ada, g_mix, w_in, g_sgu, w_sp, b_sp, w_out, g_ffn, w_up, conv_w, conv_b,
                   w_down, g_final):
    f = lambda a: np.ascontiguousarray(np.asarray(a, dtype=np.float32))
    c_bf, c_f32 = _consts()
    return {
        "w_ada": f(w_ada[:L]),
        "b_adaT": f(np.asarray(b_ada[:L]).reshape(L, 48, 128).transpose(0, 2, 1)),
        "g_mixT": f(np.asarray(g_mix[:L]).reshape(L, 8, 128).transpose(0, 2, 1)),
        "g_ffnT": f(np.asarray(g_ffn[:L]).reshape(L, 8, 128).transpose(0, 2, 1)),
        "g_finalT": f(np.asarray(g_final).reshape(8, 128).T),
        "w_in": f(w_in[:L]),
        "g_sgu": f(np.asarray(g_sgu[:L]).reshape(L, 1, 256)),
        "w_spT": f(np.asarray(w_sp[:L]).transpose(0, 1, 3, 2)),
        "b_sp": f(np.asarray(b_sp[:L]).reshape(L, 1, 512)),
        "w_out": f(w_out[:L]),
        "w_up": f(w_up[:L]),
        "conv_wT": f(np.asarray(conv_w[:L]).reshape(L, 3, 44, 128).transpose(0, 3, 2, 1)),
        "conv_bT": f(np.asarray(conv_b[:L]).reshape(L, 44, 128).transpose(0, 2, 1)),
        "w_down": f(w_down[:L]),
        "c_bf": c_bf,
        "c_f32": c_f32,
    }


def _layout_core(x_b, c_b, pos_b, T, half):
    sl = slice(half * T, (half + 1) * T)
    return {
        "xT": np.ascontiguousarray(np.asarray(x_b[sl], np.float32).T.reshape(8, 128, T)),
        "cT": np.ascontiguousarray(np.asarray(c_b, np.float32).reshape(8, 128).T),
        "pos": np.ascontiguousarray(np.asarray(pos_b[sl], np.int32)[None, :]),
        "ctxm": np.full((128, 1), float(half), np.float32),
    }


_CACHE = {}


def _get_nc(T, L, debug=False, ncores=8):
    key = (T, L, tuple(debug) if debug else None, ncores)
    if key not in _CACHE:
        b = Builder(T, L, debug=debug, ncores=ncores)
        b.build()
        print(f"[kernel] built T={T} L={L}: {b.S.nins} instructions, {b.S.nwaits} waits", flush=True)
        _CACHE[key] = b.nc
    return _CACHE[key]


def run(x, c, positions, weights, T, L, batches, debug=False):
    ncores = 2 * len(batches)
    nc = _get_nc(T, L, debug, ncores)
    shared = _layout_shared(L, **weights)
    in_maps = []
    for b in batches:
        for half in range(2):
            m = dict(shared)
            m.update(_layout_core(x[b], c[b], positions[b], T, half))
            in_maps.append(m)
    res = run_bass_kernel_spmd(nc, in_maps, core_ids=list(range(ncores)))
    return res


def kernel(x, c, positions, w_ada, b_ada, g_mix, w_in, g_sgu, w_sp, b_sp, w_out, g_ffn, w_up, conv_w,
           conv_b, w_down, g_final):
    x = np.asarray(x)
    B, T, _ = x.shape
    L = np.asarray(w_ada).shape[0]
    weights = dict(w_ada=w_ada, b_ada=b_ada, g_mix=g_mix, w_in=w_in, g_sgu=g_sgu, w_sp=w_sp, b_sp=b_sp,
                   w_out=w_out, g_ffn=g_ffn, w_up=w_up, conv_w=conv_w, conv_b=conv_b, w_down=w_down,
                   g_final=g_final)
    TH = T // 2
    res = run(x, np.asarray(c), np.asarray(positions), weights, TH, L, list(range(B)))
    out = np.empty((B, T, D), np.float32)
    for b in range(B):
        for half in range(2):
            out[b, half * TH:(half + 1) * TH] = res.results[2 * b + half]["outT"].reshape(D, TH).T
    return out
```
